# Optimizing a Trainium2 kernel written in Bass

```python
import jax, jax.numpy as jnp
from jax import lax
import numpy as np

D_MODEL = 2048
BATCH = 4
SEQ = 2048
DEPTH = 2

N_EVEN = (DEPTH + 1) // 2
N_ODD = DEPTH // 2
EPS = 1e-6

A_HEADS = 8
A_DK = 128
A_DV = 128
A_QK = A_HEADS * A_DK
A_WIDTH = A_HEADS * A_DV
A_CHUNK = 64
A_SUB = 16

B_GROUPS = 8
B_GROUP_DIM = 128
B_WIDTH = B_GROUPS * B_GROUP_DIM
B_CHUNK = 128

EVEN_IN = 2 * A_QK + 2 * A_WIDTH + 3 * B_WIDTH

C_HEADS = 16
C_Q_RANK = 512
C_KV_RANK = 512
C_NOPE = 128
C_ROPE = 64
C_QK = C_NOPE + C_ROPE
C_V = 128
C_WIDTH = C_HEADS * C_V
ROPE_THETA = 10000.0
Q_BLOCK = 128
ODD_IN = C_Q_RANK + C_KV_RANK + C_ROPE + C_WIDTH

MAX_POS_OFFSET = 4096

kernel_name = 'hybrid_hgrn2_gmlp_mla_sandwich'


def rms_norm(x, w):
    xf = x.astype(jnp.float32)
    y = xf * lax.rsqrt(jnp.mean(xf * xf, axis=-1, keepdims=True) + EPS)
    return (y * w.astype(jnp.float32)).astype(x.dtype)


def layer_norm(x, w, b):
    xf = x.astype(jnp.float32)
    mu = jnp.mean(xf, axis=-1, keepdims=True)
    xc = xf - mu
    y = xc * lax.rsqrt(jnp.mean(xc * xc, axis=-1, keepdims=True) + EPS)
    return (y * w.astype(jnp.float32) + b.astype(jnp.float32)).astype(x.dtype)


def split_sizes(z, sizes):
    offs, acc = [], 0
    for s in sizes[:-1]:
        acc += s
        offs.append(acc)
    return jnp.split(z, offs, axis=-1)


def hgrn2_chunked(q, k, v, log_f):
    bsz, T, H, K = q.shape
    V = v.shape[-1]
    C, c = A_CHUNK, A_SUB
    N, M = T // C, C // c

    def to_chunks(a):
        return a.reshape(bsz, N, C, H, a.shape[-1]).transpose(0, 3, 1, 2, 4)

    q, k, v, log_f = (to_chunks(a) for a in (q, k, v, log_f))
    b = jnp.cumsum(log_f, axis=3)
    b_last = b[:, :, :, -1, :]

    chunk_kv = jnp.einsum('bhnck,bhncv->bhnkv', k * jnp.exp(b_last[:, :, :, None, :] - b), v)

    def carry_state(state, inputs):
        decay, kv = inputs
        return decay[..., None] * state + kv, state

    _, s_prev = lax.scan(carry_state, jnp.zeros((bsz, H, K, V), q.dtype),
                         (jnp.moveaxis(jnp.exp(b_last), 2, 0), jnp.moveaxis(chunk_kv, 2, 0)))
    s_prev = jnp.moveaxis(s_prev, 0, 2)
    o_inter = jnp.einsum('bhnck,bhnkv->bhncv', q * jnp.exp(b), s_prev)

    g = (b - log_f)[:, :, :, ::c, :]
    q_sub = q.reshape(bsz, H, N, M, c, K) * jnp.exp(b.reshape(bsz, H, N, M, c, K) - g[:, :, :, :, None, :])
    sub_idx = jnp.arange(C) // c
    reach = sub_idx[None, :] <= jnp.arange(M)[:, None]
    expo = g[:, :, :, :, None, :] - b[:, :, :, None, :, :]
    k_sub = k[:, :, :, None] * jnp.exp(jnp.where(reach[:, :, None], expo, -jnp.inf))
    scores = jnp.einsum('bhnitk,bhnisk->bhnits', q_sub, k_sub).reshape(bsz, H, N, C, C)
    scores = jnp.where(jnp.tril(jnp.ones((C, C), bool)), scores, 0.0)
    o_intra = jnp.einsum('bhnts,bhnsv->bhntv', scores, v)
    o = o_inter + o_intra
    return o.transpose(0, 2, 3, 1, 4).reshape(bsz, T, H, V)


def even_mixer(h, w_in, lb, a_onorm, b_ln_w, b_ln_b, b_ws, b_bias, w_out):
    bsz, T, _ = h.shape
    f32 = jnp.float32
    z = h @ w_in
    qa, fa, ia, ga, ub, vb, gb = split_sizes(
        z, [A_QK, A_QK, A_WIDTH, A_WIDTH, B_WIDTH, B_WIDTH, B_WIDTH])

    zf = fa.astype(f32).reshape(bsz, T, A_HEADS, A_DK)
    lb = lb.reshape(A_HEADS, A_DK)
    log_f = jnp.log(lb + (1.0 - lb) * jax.nn.sigmoid(zf))
    k = (1.0 - lb) * jax.nn.sigmoid(-zf)
    q = qa.astype(f32).reshape(bsz, T, A_HEADS, A_DK)
    v = ia.astype(f32).reshape(bsz, T, A_HEADS, A_DV)
    o = hgrn2_chunked(q, k, v, log_f)
    o = rms_norm(o, a_onorm).reshape(bsz, T, A_WIDTH).astype(h.dtype)
    out_a = o * jax.nn.silu(ga)

    vg = layer_norm(vb.reshape(bsz, T, B_GROUPS, B_GROUP_DIM),
                    b_ln_w.reshape(B_GROUPS, B_GROUP_DIM), b_ln_b.reshape(B_GROUPS, B_GROUP_DIM))
    nc = T // B_CHUNK
    vg = vg.reshape(bsz, nc, B_CHUNK, B_GROUPS, B_GROUP_DIM)
    ws_causal = jnp.where(jnp.tril(jnp.ones((B_CHUNK, B_CHUNK), bool)), b_ws, 0.0)
    sv = jnp.einsum('gts,bnsgd->bntgd', ws_causal, vg) + b_bias.T[:, :, None]
    sv = sv.reshape(bsz, T, B_WIDTH)
    out_b = ub * sv * jax.nn.silu(gb)

    return jnp.concatenate([out_a, out_b], axis=-1) @ w_out


def rope_tables(positions):
    inv_freq = ROPE_THETA ** (-jnp.arange(0, C_ROPE, 2, dtype=jnp.float32) / C_ROPE)
    ang = positions.astype(jnp.float32)[..., None] * inv_freq
    return jnp.cos(ang)[:, :, None, :], jnp.sin(ang)[:, :, None, :]


def apply_rope(x, cos, sin):
    x1, x2 = jnp.split(x, 2, axis=-1)
    cos = cos.astype(x.dtype)
    sin = sin.astype(x.dtype)
    return jnp.concatenate([x1 * cos - x2 * sin, x2 * cos + x1 * sin], axis=-1)


def causal_block_attention(q, k, v):
    bsz, T, H, dqk = q.shape
    nb = T // Q_BLOCK
    scale = dqk ** -0.5
    qb = jnp.moveaxis(q.reshape(bsz, nb, Q_BLOCK, H, dqk), 1, 0)
    kf = k.astype(jnp.float32)
    k_idx = jnp.arange(T)
    neg = jnp.finfo(jnp.float32).min

    def one_block(args):
        q_blk, blk = args
        s = jnp.einsum('bqhd,bkhd->bhqk', q_blk.astype(jnp.float32), kf) * scale
        q_idx = blk * Q_BLOCK + jnp.arange(Q_BLOCK)
        s = jnp.where(k_idx[None, :] <= q_idx[:, None], s, neg)
        p = jax.nn.softmax(s, axis=-1)
        return jnp.einsum('bhqk,bkhv->bqhv', p.astype(v.dtype), v)

    out = lax.map(one_block, (qb, jnp.arange(nb)))
    return jnp.moveaxis(out, 0, 1).reshape(bsz, T, H, v.shape[-1])


def odd_mixer(h, cos, sin, w_in, q_norm, w_qb, kv_norm, w_kvb, w_out):
    bsz, T, _ = h.shape
    z = h @ w_in
    cq, ckv, kpe, gate = split_sizes(z, [C_Q_RANK, C_KV_RANK, C_ROPE, C_WIDTH])
    q = (rms_norm(cq, q_norm) @ w_qb).reshape(bsz, T, C_HEADS, C_QK)
    q_nope, q_pe = q[..., :C_NOPE], q[..., C_NOPE:]
    kv = (rms_norm(ckv, kv_norm) @ w_kvb).reshape(bsz, T, C_HEADS, C_NOPE + C_V)
    k_nope, v = kv[..., :C_NOPE], kv[..., C_NOPE:]
    q_pe = apply_rope(q_pe, cos, sin)
    k_pe = apply_rope(kpe[:, :, None, :], cos, sin)
    q = jnp.concatenate([q_nope, q_pe], axis=-1)
    k = jnp.concatenate([k_nope, jnp.broadcast_to(k_pe, (bsz, T, C_HEADS, C_ROPE))], axis=-1)
    o = causal_block_attention(q, k, v).reshape(bsz, T, C_WIDTH)
    return (o * jax.nn.silu(gate)) @ w_out


def setup_inputs(seed: int = 0) -> dict:
    key = jax.random.key(seed)
    ks = jax.random.split(key, 18)
    f32 = jnp.float32

    def normal(k, shape, scale):
        return scale * jax.random.normal(k, shape, f32)

    x = normal(ks[0], (BATCH, SEQ, D_MODEL), 1.0)
    positions = (jax.random.randint(ks[1], (BATCH, 1), 0, MAX_POS_OFFSET)
                 + jnp.arange(SEQ)[None, :]).astype(jnp.int32)
    norm_pre = 1.0 + normal(ks[2], (DEPTH, D_MODEL), 0.05)
    norm_post = 1.0 + normal(ks[3], (DEPTH, D_MODEL), 0.05)
    ev_w_in = normal(ks[4], (N_EVEN, D_MODEL, EVEN_IN), D_MODEL ** -0.5)
    ev_lb_logits = normal(ks[5], (N_EVEN + 1, A_QK), 0.1)
    ev_a_onorm = 1.0 + normal(ks[6], (N_EVEN, A_DV), 0.05)
    ev_b_ln_w = 1.0 + normal(ks[7], (N_EVEN, B_WIDTH), 0.05)
    ev_b_ln_b = normal(ks[8], (N_EVEN, B_WIDTH), 0.02)
    ev_b_ws = normal(ks[9], (N_EVEN, B_GROUPS, B_CHUNK, B_CHUNK), B_CHUNK ** -0.5)
    ev_b_bias = 1.0 + normal(ks[10], (N_EVEN, B_GROUPS, B_CHUNK), 0.02)
    ev_w_out = normal(ks[11], (N_EVEN, A_WIDTH + B_WIDTH, D_MODEL), (A_WIDTH + B_WIDTH) ** -0.5)
    od_w_in = normal(ks[12], (N_ODD, D_MODEL, ODD_IN), D_MODEL ** -0.5)
    od_q_norm = 1.0 + normal(ks[13], (N_ODD, C_Q_RANK), 0.05)
    od_w_qb = normal(ks[14], (N_ODD, C_Q_RANK, C_HEADS * C_QK), C_Q_RANK ** -0.5)
    od_kv_norm = 1.0 + normal(ks[15], (N_ODD, C_KV_RANK), 0.05)
    od_w_kvb = normal(ks[16], (N_ODD, C_KV_RANK, C_HEADS * (C_NOPE + C_V)), C_KV_RANK ** -0.5)
    od_w_out = normal(ks[17], (N_ODD, C_WIDTH, D_MODEL), C_WIDTH ** -0.5)
    return {'x': x, 'positions': positions, 'norm_pre': norm_pre, 'norm_post': norm_post,
            'ev_w_in': ev_w_in, 'ev_lb_logits': ev_lb_logits, 'ev_a_onorm': ev_a_onorm,
            'ev_b_ln_w': ev_b_ln_w, 'ev_b_ln_b': ev_b_ln_b, 'ev_b_ws': ev_b_ws,
            'ev_b_bias': ev_b_bias, 'ev_w_out': ev_w_out, 'od_w_in': od_w_in,
            'od_q_norm': od_q_norm, 'od_w_qb': od_w_qb, 'od_kv_norm': od_kv_norm,
            'od_w_kvb': od_w_kvb, 'od_w_out': od_w_out}


def reference(x, positions, norm_pre, norm_post, ev_w_in, ev_lb_logits, ev_a_onorm,
              ev_b_ln_w, ev_b_ln_b, ev_b_ws, ev_b_bias, ev_w_out, od_w_in, od_q_norm,
              od_w_qb, od_kv_norm, od_w_kvb, od_w_out):
    lower_bounds = jnp.cumsum(jax.nn.softmax(ev_lb_logits.astype(jnp.float32), axis=0), axis=0)[:N_EVEN]
    cos, sin = rope_tables(positions)
    for layer in range(DEPTH):
        j = layer // 2
        h = rms_norm(x, norm_pre[layer])
        if layer % 2 == 0:
            y = even_mixer(h, ev_w_in[j], lower_bounds[j], ev_a_onorm[j], ev_b_ln_w[j],
                           ev_b_ln_b[j], ev_b_ws[j], ev_b_bias[j], ev_w_out[j])
        else:
            y = odd_mixer(h, cos, sin, od_w_in[j], od_q_norm[j], od_w_qb[j], od_kv_norm[j],
                          od_w_kvb[j], od_w_out[j])
        x = x + rms_norm(y, norm_post[layer])
    return x
```

```python
import contextlib
import math
import numpy as np
import concourse.bass as bass
import concourse.mybir as mybir
from concourse.bass_utils import run_bass_kernel_spmd

F32 = mybir.dt.float32
BF16 = mybir.dt.bfloat16
I32 = mybir.dt.int32
AF = mybir.ActivationFunctionType
ALU = mybir.AluOpType
AX = mybir.AxisListType

ENGS = ("sp", "pe", "act", "dve", "pool")
D = 2048
TFULL = 2048
TH = 1024
NBH = TH // 512
TTH = TH // 128
EPS = 1e-6


class V:
    def __init__(self, buf, ap):
        self.buf = buf
        self.ap = ap

    def __getitem__(self, idx):
        return V(self.buf, self.ap[idx])

    def re(self, pat, **kw):
        return V(self.buf, self.ap.rearrange(pat, **kw))

    def bc(self, shape):
        return V(self.buf, self.ap.to_broadcast(shape))

    def cast(self, dt):
        return V(self.buf, self.ap.bitcast(dt))


class Buf(V):
    def __init__(self, ap, name=""):
        self.buf = self
        self.ap = ap
        self.name = name
        self.w = None
        self.r = {}


class Sched:
    def __init__(self, ring=12):
        self.ops = {e: [] for e in ENGS}
        self.count = {e: 0 for e in ENGS}
        self.seen = {e: {} for e in ENGS}
        self.rings = {"sp": [("dma", "sp", i) for i in range(ring)],
                      "pool": [("dma", "pool", i) for i in range(12)]}
        self.ring_pos = {q: 0 for q in self.rings}
        self.dma_tot = {}
        for q in self.rings:
            for k in self.rings[q]:
                self.dma_tot[k] = 0
        self.out_dmas = []
        self.epoch = {}
        self.epoch_seen = {e: True for e in ENGS}

    def mark(self, name):
        if not hasattr(self, "marks"):
            self.marks = []
        self.marks.append((name, dict(self.count)))

    def barrier(self):
        ep = {}
        for e in ENGS:
            if self.count[e] > 0:
                ep[("eng", e)] = self.count[e]
        for k, v in self.dma_tot.items():
            if v > 0:
                ep[k] = v
        self.epoch = ep
        self.epoch_seen = {e: False for e in ENGS}

    def _deps(self, eng, reads, writes):
        deps = []
        if not self.epoch_seen[eng]:
            self.epoch_seen[eng] = True
            deps.extend(self.epoch.items())
        for b in reads:
            if b.w is not None:
                deps.append(b.w)
        for b in writes:
            if b.w is not None:
                deps.append(b.w)
            deps.extend(b.r.values())
        return deps

    def _need(self, eng, deps):
        best = {}
        for k, v in deps:
            if k == ("eng", "pe") and eng == "pe":
                continue
            if best.get(k, 0) < v:
                best[k] = v
        waits = []
        for k, v in best.items():
            if self.seen[eng].get(k, 0) < v:
                self.seen[eng][k] = v
                waits.append((k, v))
        return waits

    stopped = False
    nrec = 0
    limit = 0

    def op(self, eng, fn, reads=(), writes=()):
        self.nrec += 1
        if self.stopped or (self.limit and self.nrec > self.limit):
            return None
        waits = self._need(eng, self._deps(eng, reads, writes))
        self.count[eng] += 1
        me = (("eng", eng), self.count[eng])
        self.ops[eng].append((waits, fn, me, 1))
        for b in reads:
            b.r[me[0]] = me
        for b in writes:
            b.w = me
            b.r = {}
        return me

    def dma(self, q, fn, reads=(), writes=(), is_output=False, force=False):
        self.nrec += 1
        if not force and (self.stopped or (self.limit and self.nrec > self.limit)):
            return None
        deps = self._deps(q, reads, writes)
        ring = self.rings[q]
        k = ring[self.ring_pos[q] % len(ring)]
        self.ring_pos[q] += 1
        if self.dma_tot[k] > 0:
            deps.append((k, self.dma_tot[k]))
        waits = self._need(q, deps)
        self.dma_tot[k] += 16
        me = (k, self.dma_tot[k])
        self.ops[q].append((waits, fn, me, 16))
        for b in reads:
            b.r[k] = me
        for b in writes:
            b.w = me
            b.r = {}
        if is_output:
            self.out_dmas.append(me)
        return me

    def emit(self, nc, stack):
        sems = {}
        for e in ENGS:
            sems[("eng", e)] = stack.enter_context(nc.semaphore("s_" + e))
        for q in self.rings:
            for k in self.rings[q][: min(len(self.rings[q]), self.ring_pos[q])]:
                sems[k] = stack.enter_context(nc.semaphore("d_%s_%d" % (k[1], k[2])))
        final_waits = {}
        for k, v in self.out_dmas:
            final_waits[k] = max(final_waits.get(k, 0), v)
        for k, v in self.dma_tot.items():
            if v > 0:
                final_waits[k] = max(final_waits.get(k, 0), v)
        for e in ENGS:
            if e != "sp" and self.count[e] > 0:
                final_waits[("eng", e)] = self.count[e]

        def replay(name, eng):
            for waits, fn, me, amt in self.ops[name]:
                for k, v in waits:
                    eng.wait_ge(sems[k], v)
                ins = fn(eng)
                ins.then_inc(sems[me[0]], amt)
            if name == "sp":
                for k, v in final_waits.items():
                    eng.wait_ge(sems[k], v)

        block = stack.enter_context(nc.Block())

        @block.sync
        def _(e):
            replay("sp", e)

        @block.tensor
        def _(e):
            replay("pe", e)

        @block.scalar
        def _(e):
            replay("act", e)

        @block.vector
        def _(e):
            replay("dve", e)

        @block.gpsimd
        def _(e):
            replay("pool", e)


class Arena:
    def __init__(self, nc, stack, kbytes):
        self.n = kbytes * 256
        self.t = stack.enter_context(nc.sbuf_tensor("arena", [128, self.n], F32))
        self.off = 0

    def alloc(self, name, shape, dt=F32, parts=128):
        nel = 1
        for s in shape:
            nel *= s
        nbytes = nel * (2 if dt == BF16 else 4)
        ncol = (nbytes + 3) // 4
        assert self.off + ncol <= self.n, "arena overflow at %s: %d + %d > %d" % (name, self.off, ncol, self.n)
        ap = self.t[0:parts, self.off:self.off + ncol]
        off0 = self.off
        self.off += ncol
        if dt != F32:
            ap = ap.bitcast(dt)
        if len(shape) == 2:
            ap = ap.rearrange("p (a b) -> p a b", a=shape[0])
        elif len(shape) == 3:
            ap = ap.rearrange("p (a b c) -> p a b c", a=shape[0], b=shape[1])
        b = Buf(ap, name)
        b.off = off0
        b.ncol = ncol
        return b


class _Stop(Exception):
    pass


class _Y:
    def __init__(self, ytiles):
        self.ytiles = ytiles

    def __getitem__(self, idx):
        p, i, n = idx
        return self.ytiles[i][:, n]


def build_program(layers=(0, 1), dbg=None):
    do0 = 0 in layers
    do1 = 1 in layers
    nc = bass.Bass("TRN2", target_bir_lowering=False)
    S = Sched()
    import os
    S.limit = int(os.environ.get("KLIMIT", "0"))

    def din(name, shape, dt=F32):
        return nc.dram_tensor(name, shape, dt, kind="ExternalInput").ap()

    x_d = din("x", [TFULL, D])
    out_d = nc.dram_tensor("out", [TFULL, D], F32, kind="ExternalOutput").ap()
    if do0 and do1:
        x1_d = nc.dram_tensor("x1s", [TFULL, D], F32).ap()
    elif do0:
        x1_d = out_d
    else:
        x1_d = x_d
    npre_d = din("norm_pre", [2, D])
    npost_d = din("norm_post", [2, D])
    if do0:
        ev_w_in = din("ev_w_in", [D, 7168])
        ev_lb = din("ev_lb_logits", [16, 128])
        ev_onorm = din("ev_a_onorm", [1, 128])
        ev_lnw = din("ev_b_ln_w", [8, 128])
        ev_lnb = din("ev_b_ln_b", [8, 128])
        ev_ws = din("ev_b_ws", [8, 128, 128])
        ev_bias = din("ev_b_bias", [1024])
        ev_w_out = din("ev_w_out", [D, D])
    if do1:
        pos_d = din("positions", [TFULL], I32)
        od_w_in = din("od_w_in", [D, 3136])
        od_qn = din("od_q_norm", [4, 128])
        od_wqb = din("od_w_qb", [512, 3072])
        od_kvn = din("od_kv_norm", [4, 128])
        od_wkvb = din("od_w_kvb", [512, 4096])
        od_w_out = din("od_w_out", [D, D])
        kvs_d = nc.dram_tensor("kvs", [16, 2, 128, TH], BF16).ap()

    dbg_d = nc.dram_tensor("dbg", [128, 16 * TH], BF16, kind="ExternalOutput").ap() if (dbg and not dbg.startswith("L1:")) else None
    x1_tiles = [Buf(None, "x1t%d" % i) for i in range(TFULL // 128)]
    kvs_bufs = [Buf(None, "kvs%d" % i) for i in range(16)]

    with contextlib.ExitStack() as st:
        A = Arena(nc, st, 207)
        pb = [Buf(st.enter_context(nc.psum_tensor("pb%d" % i, [128, 512], F32))[:, :], "pb%d" % i) for i in range(8)]

        def mm(out, lhsT, rhs, start=True, stop=True, reads=()):
            S.op("pe", lambda e: e.matmul(out.ap, lhsT=lhsT.ap, rhs=rhs.ap, start=start, stop=stop),
                 reads=[lhsT.buf, rhs.buf, *reads], writes=[out.buf])

        def tr(out, in_, ident):
            S.op("pe", lambda e: e.transpose(out=out.ap, in_=in_.ap, identity=ident.ap),
                 reads=[in_.buf, ident.buf], writes=[out.buf])

        def act(out, in_, func, bias=None, scale=None, accum=None, reads=(), junk_out=False):
            kw = {}
            rd = [in_.buf, *reads]
            wr = [] if junk_out else [out.buf]
            if bias is not None:
                if isinstance(bias, V):
                    kw["bias"] = bias.ap
                    rd.append(bias.buf)
                else:
                    kw["bias"] = float(bias)
            if scale is not None:
                if isinstance(scale, V):
                    kw["scale"] = scale.ap
                    rd.append(scale.buf)
                else:
                    kw["scale"] = float(scale)
            if accum is not None:
                kw["accum_out"] = accum.ap
                wr.append(accum.buf)
            S.op("act", lambda e: e.activation(out=out.ap, in_=in_.ap, func=func, **kw), reads=rd, writes=wr)

        def tt(eng, out, in0, in1, op):
            S.op(eng, lambda e: e.tensor_tensor(out=out.ap, in0=in0.ap, in1=in1.ap, op=op),
                 reads=[in0.buf, in1.buf], writes=[out.buf])

        def ts(eng, out, in0, s1, s2, op0, op1=None):
            rd = [in0.buf]
            a1 = s1
            a2 = s2
            if isinstance(s1, V):
                rd.append(s1.buf)
                a1 = s1.ap
            if isinstance(s2, V):
                rd.append(s2.buf)
                a2 = s2.ap
            if op1 is None:
                S.op(eng, lambda e: e.tensor_scalar(out=out.ap, in0=in0.ap, scalar1=a1, scalar2=None, op0=op0),
                     reads=rd, writes=[out.buf])
            else:
                S.op(eng, lambda e: e.tensor_scalar(out=out.ap, in0=in0.ap, scalar1=a1, scalar2=a2, op0=op0, op1=op1),
                     reads=rd, writes=[out.buf])

        def stt(out, in0, scalar, in1, op0, op1):
            rd = [in0.buf, in1.buf]
            a = scalar
            if isinstance(scalar, V):
                rd.append(scalar.buf)
                a = scalar.ap
            S.op("dve", lambda e: e.scalar_tensor_tensor(out=out.ap, in0=in0.ap, scalar=a, in1=in1.ap, op0=op0, op1=op1),
                 reads=rd, writes=[out.buf])

        def cp(eng, out, in_):
            if eng == "act":
                S.op("act", lambda e: e.activation(out=out.ap, in_=in_.ap, func=AF.Copy), reads=[in_.buf], writes=[out.buf])
            else:
                S.op(eng, lambda e: e.tensor_copy(out=out.ap, in_=in_.ap), reads=[in_.buf], writes=[out.buf])

        def recip(out, in_):
            S.op("dve", lambda e: e.reciprocal(out=out.ap, in_=in_.ap), reads=[in_.buf], writes=[out.buf])

        def red(out, in_):
            S.op("dve", lambda e: e.tensor_reduce(out=out.ap, in_=in_.ap, axis=AX.X, op=ALU.add),
                 reads=[in_.buf], writes=[out.buf])

        def memset(eng, out, val):
            S.op(eng, lambda e: e.memset(out.ap, val), writes=[out.buf])

        def dma_in(q, out, src_ap, reads=()):
            S.dma(q, lambda e: e.dma_start(out=out.ap, in_=src_ap), reads=list(reads), writes=[out.buf])

        def dma_out(q, dst_ap, in_, writes=(), is_output=False):
            S.dma(q, lambda e: e.dma_start(out=dst_ap, in_=in_.ap), reads=[in_.buf], writes=list(writes), is_output=is_output)

        identf = A.alloc("identf", [128], F32)
        identb = A.alloc("identb", [128], BF16)
        tri64 = A.alloc("tri64", [64], F32)
        tri128b = A.alloc("tri128b", [128], BF16)
        wsmask = A.alloc("wsmask", [128], F32)
        ones1 = A.alloc("ones1", [128], F32)
        onesb = A.alloc("onesb", [128], BF16)
        cmask = A.alloc("cmask", [512], BF16)
        pst = A.alloc("pst", [128], F32)
        cols = A.alloc("cols", [48], F32)
        tmpc = A.alloc("tmpc", [128], F32)

        memset("pool", identf, 1.0)
        S.op("pool", lambda e: e.affine_select(out=identf.ap, in_=identf.ap, pattern=[[-1, 128]], compare_op=ALU.is_equal,
                                               fill=0.0, base=0, channel_multiplier=1), reads=[identf], writes=[identf])
        cp("dve", identb, identf)
        memset("pool", tri64, 1.0)
        S.op("pool", lambda e: e.affine_select(out=tri64[0:64, :].ap, in_=tri64[0:64, :].ap, pattern=[[1, 64]],
                                               compare_op=ALU.is_ge, fill=0.0, base=0, channel_multiplier=-1),
             reads=[tri64], writes=[tri64])
        memset("pool", tmpc, 1.0)
        S.op("pool", lambda e: e.affine_select(out=tmpc.ap, in_=tmpc.ap, pattern=[[1, 128]], compare_op=ALU.is_ge,
                                               fill=0.0, base=0, channel_multiplier=-1), reads=[tmpc], writes=[tmpc])
        cp("dve", tri128b, tmpc)
        memset("pool", wsmask, 1.0)
        S.op("pool", lambda e: e.affine_select(out=wsmask.ap, in_=wsmask.ap, pattern=[[-1, 128]], compare_op=ALU.is_ge,
                                               fill=0.0, base=0, channel_multiplier=1), reads=[wsmask], writes=[wsmask])
        memset("pool", ones1, 1.0)
        memset("pool", onesb, 1.0)
        memset("pool", cmask, 1.0)
        memset("pool", cmask.re("p (j t) -> p j t", j=8)[:, :, 0:1], 0.0)
        memset("pool", pst, 0.0)

        if do0:
            dma_in("sp", pst[0:16, :], ev_lb)
            dma_in("sp", pst[16:17, :], ev_onorm)
            dma_in("sp", pst[17:25, :], ev_lnw)
            dma_in("sp", pst[25:33, :], ev_lnb)
        if do1:
            dma_in("sp", pst[33:37, :], od_qn)
            dma_in("sp", pst[37:41, :], od_kvn)
        tr(pb[0][:, 0:48], pst[0:48, :], identf[0:48, 0:48])
        cp("dve", cols, pb[0][:, 0:48])
        C_LB, C_ON, C_LNW, C_LNB, C_QN, C_KVN = 0, 16, 17, 25, 33, 37

        wbc = A.alloc("wbc", [D], F32)
        hT = A.alloc("hT", [16, TH], BF16)
        hT_tiles = [Buf(hT.ap, "hT%d" % i) for i in range(TTH)]
        outT = A.alloc("outT", [16, TH], BF16)
        outT_b = [[Buf(outT.ap, "outT%d_%d" % (c, b)) for b in range(NBH)] for c in range(16)]
        wslab = [A.alloc("wslab%d" % i, [16, 512], BF16) for i in range(2)]
        mark_layer = A.off
        if dbg and not dbg.startswith("L1:"):
            memset("pool", outT, 0.0)

        bank_rot = [0]

        def nextbank():
            b = pb[bank_rot[0] % 3]
            bank_rot[0] += 1
            return b

        def inproj(dst, w, col0, blk, m=128):
            tok = slice(blk * 512, (blk + 1) * 512)
            tiles = hT_tiles[4 * blk:4 * blk + 4]
            for c in range(16):
                mm(dst, w[:, c, col0:col0 + m], V(tiles[0], hT.ap[:, c, tok]), start=(c == 0), stop=(c == 15),
                   reads=tiles[1:])

        def phaseA(layer, half, XA, preloaded=0):
            src = x_d if layer == 0 else x1_d
            dma_in("sp", wbc, npre_d[layer, :].partition_broadcast(128))
            ss8 = XA["ss8"]
            junk = XA["junk"]
            k = 0
            def tile_out(i, j):
                xt = XA["xt"][j]
                hb = XA["hb"][i % 2]
                stt(hb, xt, ss8[:, i:i + 1], wbc, ALU.mult, ALU.mult)
                tpa = pb[(2 * kk_[0]) % 8].cast(BF16).re("p (c d) -> p c d", c=8)
                tpb = pb[(2 * kk_[0] + 1) % 8].cast(BF16).re("p (c d) -> p c d", c=8)
                kk_[0] += 1
                for c in range(16):
                    dst = tpa if c < 8 else tpb
                    tr(dst[:, c % 8, :], hb[:, c * 128:(c + 1) * 128], identb)
                tl = hT_tiles[i]
                cp("act", V(tl, hT.ap[:, 0:8, i * 128:(i + 1) * 128]), tpa)
                cp("dve", V(tl, hT.ap[:, 8:16, i * 128:(i + 1) * 128]), tpb)

            def rstd_of(sl):
                ts("dve", sl, sl, 1.0 / D, EPS, ALU.mult, ALU.add)
                act(sl, sl, AF.Sqrt)
                recip(sl, sl)

            kk_ = [0]
            for wv_ in range(TTH // 4):
                for j in range(4):
                    i = wv_ * 4 + j
                    gi = half * TTH + i
                    xt = XA["xt"][j]
                    rd = [x1_tiles[gi]] if layer == 1 else []
                    if i >= preloaded:
                        dma_in("sp", xt, src[gi * 128:(gi + 1) * 128, :], reads=rd)
                    act(junk, xt, AF.Square, accum=ss8[:, i:i + 1], junk_out=True)
                    if i == 0:
                        rstd_of(ss8[:, 0:1])
                if wv_ == 0:
                    tile_out(0, 0)
                    rstd_of(ss8[:, 1:4])
                    first = 1
                else:
                    rstd_of(ss8[:, wv_ * 4:(wv_ + 1) * 4])
                    first = 0
                for j in range(first, 4):
                    tile_out(wv_ * 4 + j, j)

        def phaseC_preload(w_out_d):
            wv = w_out_d.rearrange("(c p) n -> p c n", p=128)
            for nb in range(2):
                dma_in("pool", wslab[nb % 2], wv[:, :, nb * 512:(nb + 1) * 512])

        def phaseC(layer, half, XC, w_out_d, dst_d, is_final, after=None):
            src = x_d if layer == 0 else x1_d
            dma_in("sp", wbc, npost_d[layer, :].partition_broadcast(128))
            yacc = XC["yacc"]
            ss8 = XC["ss8"]
            ssq = XC["ssq"]
            junk = XC["junk"]
            wv = w_out_d.rearrange("(c p) n -> p c n", p=128)
            nxt = len(XC["xt"])

            def load_x(i):
                gi = half * TTH + i
                rd = [x1_tiles[gi]] if layer == 1 else []
                dma_in("sp", XC["xt"][i % nxt], src[gi * 128:(gi + 1) * 128, :], reads=rd)

            for i in range(nxt):
                load_x(i)
            k = 0
            for nb in range(4):
                w = wslab[nb % 2]
                if nb >= 2:
                    dma_in("pool", w, wv[:, :, nb * 512:(nb + 1) * 512])
                for i in range(TTH):
                    bank = pb[k % 8]
                    k += 1
                    for c in range(16):
                        mm(bank, V(outT_b[c][i // 4], outT.ap[:, c, i * 128:(i + 1) * 128]), w[:, c, :],
                           start=(c == 0), stop=(c == 15))
                    act(junk, bank, AF.Square, accum=ssq[:, i, nb:nb + 1], junk_out=True)
                    yv = yacc[:, i, nb * 512:(nb + 1) * 512]
                    wv_ = wbc[:, nb * 512:(nb + 1) * 512]
                    S.op("dve", (lambda o, a, b: (lambda e: e.tensor_tensor(out=o, in0=a, in1=b, op=ALU.mult)))(yv.ap, bank.ap, wv_.ap),
                         reads=[bank, wbc, ssq], writes=[yv.buf])
            tt("dve", ss8, ssq[:, :, 0], ssq[:, :, 1], ALU.add)
            tt("dve", ss8, ss8, ssq[:, :, 2], ALU.add)
            tt("dve", ss8, ss8, ssq[:, :, 3], ALU.add)
            ts("dve", ss8, ss8, 1.0 / D, EPS, ALU.mult, ALU.add)
            act(ss8, ss8, AF.Sqrt)
            recip(ss8, ss8)
            for i in range(TTH):
                gi = half * TTH + i
                xt = XC["xt"][i % nxt]
                yt = yacc[:, i, :]
                stt(yt, yt, ss8[:, i:i + 1], xt, ALU.mult, ALU.add)
                wr = [x1_tiles[gi]] if (layer == 0) else []
                dma_out("sp", dst_d[gi * 128:(gi + 1) * 128, :], yt, writes=wr, is_output=is_final)
                if i + nxt < TTH:
                    load_x(i + nxt)
                if i == nxt - 1 and after is not None:
                    after()

        def dump(buf16):
            S.dma("sp", lambda e: e.dma_start(out=dbg_d, in_=buf16.ap.rearrange("p c t -> p (c t)")), reads=[buf16] + hT_tiles + [b for r in outT_b for b in r], is_output=True, force=True)
            S.stopped = True

        if do0:
            lbc = A.alloc("lbc", [8], F32)
            omlc = A.alloc("omlc", [8], F32)
            nomlc = A.alloc("nomlc", [8], F32)
            wsT = A.alloc("wsT", [8, 128], BF16)
            Cg = A.alloc("Cg", [8, 128], F32)
            biasbc = A.alloc("biasbc", [8, 128], F32)
            Sfin = A.alloc("Sfin", [8, 128], F32)
            wsraw = [A.alloc("wsraw%d" % i, [128], F32) for i in range(2)]
            wsm = [A.alloc("wsm%d" % i, [128], BF16) for i in range(2)]
            mark_x = A.off
            xt4 = [A.alloc("xt4_%d" % i, [D], F32) for i in range(6)]
            mark_xa = A.off
            for j in range(4):
                dma_in("sp", xt4[j], x_d[j * 128:(j + 1) * 128, :])

            tt("dve", lbc, cols[:, C_LB:C_LB + 8], cols[:, C_LB + 8:C_LB + 16], ALU.subtract)
            act(lbc, lbc, AF.Sigmoid)
            ts("dve", omlc, lbc, -1.0, 1.0, ALU.mult, ALU.add)
            ts("dve", nomlc, omlc, -1.0, None, ALU.mult)
            dma_in("sp", biasbc, ev_bias.partition_broadcast(128).rearrange("p (g t) -> p g t", g=8))
            for g in range(8):
                dma_in("sp", wsraw[g % 2], ev_ws[g])
                tt("dve", wsm[g % 2], wsraw[g % 2], wsmask, ALU.mult)
                tpw = pb[1].cast(BF16)[:, 0:128]
                tr(tpw, wsm[g % 2], identb)
                cp("act", wsT[:, g, :], tpw)
                mm(pb[2][:, 0:128], onesb, wsT[:, g, :])
                stt(Cg[:, g, :], pb[2][:, 0:128], cols[:, C_LNB + g:C_LNB + g + 1], biasbc[:, g, :], ALU.mult, ALU.add)

            w0v = ev_w_in.rearrange("(c p) n -> p c n", p=128)

            def preload_next_A0():
                for j in range(4):
                    dma_in("sp", xt4[2 + j], x_d[(TTH + j) * 128:(TTH + j + 1) * 128, :])

            for half in range(2):
                slab_of = {}
                slab_cnt = [0]

                def load_slab(kind, idx):
                    key = (kind, idx)
                    if key in slab_of:
                        return
                    w = wslab[slab_cnt[0] % 2]
                    slab_cnt[0] += 1
                    slab_of[key] = w
                    if kind == "h":
                        for gi_, base in enumerate((1024, 0, 2048, 3072)):
                            c0 = base + idx * 128
                            dma_in("pool", w[:, :, gi_ * 128:(gi_ + 1) * 128], w0v[:, :, c0:c0 + 128])
                    else:
                        for gi_, base in enumerate((4096, 5120, 6144, 6144) if os.environ.get("G4") else (4096, 5120, 6144)):
                            c0 = base + idx * 128
                            dma_in("pool", w[:, :, gi_ * 128:(gi_ + 1) * 128], w0v[:, :, c0:c0 + 128])

                load_slab("h", 0)
                load_slab("h", 1)
                if half > 0:
                    S.barrier()
                A.off = mark_xa
                XA = {"xt": (xt4[0:4] if half == 0 else xt4[2:6]),
                      "hb": [A.alloc("hbA%d" % i, [D], BF16) for i in range(2)],
                      "junk": A.alloc("junkA", [D], BF16),
                      "ss8": A.alloc("ss8A", [TTH], F32)}
                S.mark("L0h%d:A" % half)
                phaseA(0, half, XA, preloaded=4)
                S.mark("L0h%d:B" % half)
                if dbg == "A":
                    dump(hT)
                S.barrier()
                A.off = mark_x
                f2 = lambda n: A.alloc(n, [512], F32)
                b2 = lambda n: A.alloc(n, [512], BF16)
                sig, logf, bb, kk, ek, dd = f2("sig"), f2("logf"), f2("bb"), f2("kk"), f2("ek"), f2("dd")
                ktil = [b2("ktil%d" % i) for i in range(2)]
                kdec = [b2("kdec%d" % i) for i in range(2)]
                vT = [b2("vT%d" % i) for i in range(2)]
                kdT = A.alloc("kdT", [8, 128], BF16, parts=64)
                eq = [f2("eq%d" % i) for i in range(2)]
                qtil = [b2("qtil%d" % i) for i in range(3)]
                sg = [f2("sg%d" % i) for i in range(3)]
                vtok = [A.alloc("vtok%d" % i, [8, 128], BF16, parts=64) for i in range(2)]
                scT = [A.alloc("scT%d" % i, [8, 64], BF16, parts=64) for i in range(2)]
                SbA = [A.alloc("SbA%d" % i, [8, 128], BF16) for i in range(3)]
                Sf = [A.alloc("Sf%d" % i, [128], F32) for i in range(2)]
                osqb, sd, t1 = b2("osqb"), f2("sd"), f2("t1")
                sqv = f2("sqv")
                vTf = f2("vTf")
                st4 = [A.alloc("st4_%d" % i, [4], F32) for i in range(6)]
                vn = A.alloc("vn", [4, 128], BF16)
                svs, sgb, usg = f2("svs"), f2("sgb"), f2("usg")

                units = []
                for hd in range(8):
                    for blk in range(NBH):
                        units.append(("h", hd, blk))
                for g in range(8):
                    for blk in range(NBH):
                        units.append(("g", g, blk))

                def next_slab_key(ui):
                    kind, idx, blk = units[ui]
                    for uj in range(ui + 1, len(units)):
                        if (units[uj][0], units[uj][1]) != (kind, idx):
                            return (units[uj][0], units[uj][1])
                    return None

                def front_a(ui):
                    _, hd, blk = units[ui]
                    par = ui % 2
                    p3 = ui % 3
                    gblk = half * NBH + blk
                    load_slab("h", hd)
                    if blk == 0:
                        nk = next_slab_key(ui)
                        if nk is not None:
                            load_slab(*nk)
                    w = slab_of[("h", hd)]
                    lb_c = lbc[:, hd:hd + 1]
                    oml_c = omlc[:, hd:hd + 1]
                    noml_c = nomlc[:, hd:hd + 1]
                    pf = nextbank()
                    inproj(pf, w, 0, blk)
                    act(sig, pf, AF.Sigmoid)
                    pg = nextbank()
                    inproj(pg, w, 384, blk)
                    act(sg[p3], pg, AF.Silu)
                    pq = nextbank()
                    inproj(pq, w, 128, blk)
                    act(logf, sig, AF.Ln, bias=lb_c, scale=oml_c)
                    ts("pool", kk, sig, noml_c, oml_c, ALU.mult, ALU.add)
                    S.op("dve", lambda e: e.tensor_tensor_scan(out=bb.ap, data0=cmask.ap, data1=logf.ap, initial=0.0,
                                                               op0=ALU.mult, op1=ALU.add),
                         reads=[cmask, logf], writes=[bb])
                    act(eq[par], bb, AF.Exp)
                    act(ek, bb, AF.Exp, scale=-1.0)
                    bb3 = bb.re("p (j t) -> p j t", j=8)
                    tt("dve", dd.re("p (j t) -> p j t", j=8), bb3, bb3[:, :, 63:64].bc([128, 8, 64]), ALU.subtract)
                    act(dd, dd, AF.Exp, scale=-1.0)
                    tt("dve", qtil[p3], pq, eq[par], ALU.mult)
                    pi_ = nextbank()
                    inproj(pi_, w, 256, blk)
                    tt("pool", ktil[par], kk, ek, ALU.mult)
                    tt("pool", kdec[par], kk, dd, ALU.mult)
                    cp("act", vT[par], pi_)

                def front_b(ui, mid=None):
                    _, hd, blk = units[ui]
                    par = ui % 2
                    p3 = ui % 3
                    gblk = half * NBH + blk
                    tpv = pb[3][0:64, :].cast(BF16).re("p (j d) -> p j d", j=8)
                    for j in range(8):
                        tr(tpv[:, j, :], vT[par][:, j * 64:(j + 1) * 64], identb)
                    cp("dve", vtok[par], tpv)
                    for j in range(8):
                        tr(tpv[:, j, :], kdec[par][:, j * 64:(j + 1) * 64], identb)
                    cp("dve", kdT, tpv)
                    psc = pb[4][0:64, :].re("p (j t) -> p j t", j=8)
                    for j in range(8):
                        mm(psc[:, j, :], ktil[par][:, j * 64:(j + 1) * 64], qtil[p3][:, j * 64:(j + 1) * 64])
                    tt("dve", scT[par], psc, tri64[0:64, :].re("p (o t) -> p o t", o=1).bc([64, 8, 64]), ALU.mult)
                    pkv = pb[5].re("p (j d) -> p j d", j=4)
                    last_unit_of_head = (blk == NBH - 1)
                    for hh in range(2):
                        for j4 in range(4):
                            j = hh * 4 + j4
                            mm(pkv[:, j4, :], kdT[:, j, :], vtok[par][:, j, :])
                        for j4 in range(4):
                            j = hh * 4 + j4
                            c = gblk * 8 + j
                            dcol = eq[par][:, j * 64 + 63:j * 64 + 64]
                            is_last = last_unit_of_head and j == 7
                            if is_last:
                                dst = Sfin[:, hd, :]
                            else:
                                dst = Sf[(c + 1) % 2]
                            if c == 0:
                                cp("dve", dst, pkv[:, j4, :])
                            else:
                                if j == 0 and blk == 0:
                                    prev = Sfin[:, hd, :]
                                else:
                                    prev = Sf[c % 2]
                                stt(dst, prev, dcol, pkv[:, j4, :], ALU.mult, ALU.add)
                            if not is_last:
                                if j < 7:
                                    cp("pool", SbA[ui % 3][:, j + 1, :], dst)
                                else:
                                    cp("pool", SbA[(ui + 1) % 3][:, 0, :], dst)
                        if hh == 0 and mid is not None:
                            mid()

                def pre_h(ui):
                    _, hd, blk = units[ui]
                    if half == 1 and blk == 0:
                        cp("pool", SbA[ui % 3][:, 0, :], Sfin[:, hd, :])

                def back_o(ui):
                    _, hd, blk = units[ui]
                    par = ui % 2
                    p3 = ui % 3
                    gblk = half * NBH + blk
                    po = pb[6]
                    for j in range(8):
                        c = gblk * 8 + j
                        first = (c == 0)
                        mm(po[:, j * 64:(j + 1) * 64], vtok[par][:, j, :], scT[par][:, j, :], start=True, stop=first)
                        if not first:
                            mm(po[:, j * 64:(j + 1) * 64], SbA[ui % 3][:, j, :], qtil[p3][:, j * 64:(j + 1) * 64],
                               start=False, stop=True)

                def back_n(ui):
                    _, hd, blk = units[ui]
                    par = ui % 2
                    p3 = ui % 3
                    po = pb[6]
                    act(osqb, po, AF.Square)
                    mm(pb[7], onesb, osqb)
                    act(sd, pb[7], AF.Ln, bias=EPS, scale=1.0 / 128)
                    act(sd, sd, AF.Exp, scale=-0.5)
                    stt(t1, po, cols[:, C_ON:C_ON + 1], sd, ALU.mult, ALU.mult)
                    tt("dve", V(outT_b[hd][blk], outT.ap[:, hd, blk * 512:(blk + 1) * 512]), t1, sg[p3], ALU.mult)

                def front_g(ui):
                    _, g, blk = units[ui]
                    if os.environ.get("KLIMIT"):
                        print("front_g start nrec", S.nrec)
                    load_slab("g", g)
                    if blk == 0 and not os.environ.get("NOPF"):
                        nk = next_slab_key(ui)
                        if nk is not None:
                            load_slab(*nk)
                    w = slab_of[("g", g)]
                    pvb = nextbank()
                    pv = pvb.re("p (j d) -> p j d", j=4)
                    for tl in range(4):
                        tile = hT_tiles[4 * blk + tl]
                        t0 = blk * 512 + tl * 128
                        for c in range(16):
                            mm(pv[:, tl, :], V(tile, hT.ap[:, c, t0:t0 + 128]), w[:, c, 128:256], start=(c == 0), stop=(c == 15))
                    s1, s2, mean, msq, rstd, nmr = st4
                    for tl in range(4):
                        act(vTf[:, tl * 128:(tl + 1) * 128], pv[:, tl, :], AF.Identity, accum=s1[:, tl:tl + 1], junk_out=True)
                        act(sqv[:, tl * 128:(tl + 1) * 128], pv[:, tl, :], AF.Square, accum=s2[:, tl:tl + 1], junk_out=True)
                    pu = nextbank()
                    inproj(pu, w, 0, blk)
                    ts("dve", mean, s1, 1.0 / 128, None, ALU.mult)
                    tt("dve", msq, mean, mean, ALU.mult)
                    stt(msq, s2, 1.0 / 128, msq, ALU.mult, ALU.subtract)
                    act(rstd, msq, AF.Sqrt, bias=EPS)
                    recip(rstd, rstd)
                    stt(nmr, mean, -1.0, rstd, ALU.mult, ALU.mult)
                    for tl in range(4):
                        act(vn[:, tl, :], pv[:, tl, :], AF.Identity, bias=nmr[:, tl:tl + 1], scale=rstd[:, tl:tl + 1])
                    pgt = nextbank()
                    inproj(pgt, w, 256, blk)
                    act(sgb, pgt, AF.Silu)
                    tt("dve", usg, pu, sgb, ALU.mult)
                    psv = pb[4]
                    for tl in range(4):
                        mm(psv[:, tl * 128:(tl + 1) * 128], vn[:, tl, :], wsT[:, g, :])
                    stt(svs.re("p (j t) -> p j t", j=4), psv.re("p (j t) -> p j t", j=4), cols[:, C_LNW + g:C_LNW + g + 1],
                        Cg[:, g, :].re("p (o t) -> p o t", o=1).bc([128, 4, 128]), ALU.mult, ALU.add)
                    tt("dve", V(outT_b[8 + g][blk], outT.ap[:, 8 + g, blk * 512:(blk + 1) * 512]), svs, usg, ALU.mult)

                n_u = len(units)
                if dbg == "B1":
                    units = units[:2] + units[16:18]
                    n_u = len(units)
                if dbg and dbg.startswith("B:"):
                    lo, hi = dbg[2:].split("-")
                    units = units[int(lo):int(hi)]
                    n_u = len(units)
                if dbg and dbg.startswith("BL:"):
                    units = [units[int(t)] for t in dbg[3:].split(",")]
                    n_u = len(units)
                for ui in range(n_u + 2):
                    if ui < n_u:
                        if units[ui][0] == "h":
                            front_a(ui)
                        else:
                            front_g(ui)
                    has_c = (2 <= ui and units[ui - 2][0] == "h")
                    if 1 <= ui <= n_u and units[ui - 1][0] == "h":
                        pre_h(ui - 1)
                        front_b(ui - 1, mid=((lambda u=ui - 2: back_o(u)) if has_c else None))
                        if has_c:
                            back_n(ui - 2)
                    elif has_c:
                        back_o(ui - 2)
                        back_n(ui - 2)

                if dbg and dbg[0] == "B":
                    dump(outT)
                phaseC_preload(ev_w_out)
                S.barrier()
                A.off = mark_xa
                XC = {"yacc": None,
                      "xt": xt4,
                      "ss8": A.alloc("ss8C", [TTH], F32),
                      "ssq": A.alloc("ssqC", [TTH, 4], F32),
                      "junk": A.alloc("junkC", [512], BF16)}
                yup = A.alloc("yaccU", [TTH // 2, D], F32)
                ytiles = []
                for i in range(TTH):
                    if i < TTH // 2:
                        ytiles.append(Buf(A.t[:, hT.off + i * D:hT.off + (i + 1) * D], "yaccL%d" % i))
                    else:
                        ytiles.append(Buf(yup.ap[:, i - TTH // 2, :], "yaccU%d" % i))
                XC["yacc"] = _Y(ytiles)
                S.mark("L0h%d:C" % half)
                if os.environ.get("KLIMIT"):
                    print("L0 C start nrec", S.nrec)
                phaseC(0, half, XC, ev_w_out, x1_d, is_final=(not do1),
                       after=(preload_next_A0 if half == 0 else None))
            A.off = mark_layer

        if do1 and not S.stopped:
            A.off = mark_layer
            kpeT = A.alloc("kpeT", [TFULL], BF16, parts=64)
            nc._kpeT_off = kpeT.off
            invrow = A.alloc("invrow", [64], F32, parts=1)
            invcol = A.alloc("invcol", [1], F32, parts=64)
            lat_off = A.off
            cqsT = A.alloc("cqsT", [4, TH], BF16)
            ckvsT = A.alloc("ckvsT", [4, TH], BF16)
            rstdq = A.alloc("rstdq", [TH], F32)
            rstdkv = A.alloc("rstdkv", [TH], F32)
            Ctab = A.alloc("Ctab", [TH], F32, parts=64)
            Stab = A.alloc("Stab", [TH], F32, parts=64)
            assert A.off - lat_off == 8192
            mark_x1 = A.off

            for p in range(64):
                val = float(np.float32(10000.0 ** (-(p % 32) / 32.0))) / (2.0 * math.pi)
                memset("pool", invrow[0:1, p:p + 1], val)
            inv_d = nc.dram_tensor("invs", [64], F32).ap()
            inv_b = Buf(None, "invs")
            dma_out("sp", inv_d.rearrange("(o n) -> o n", o=1), invrow, writes=[inv_b])
            dma_in("sp", invcol, inv_d.rearrange("(p x) -> p x", x=1), reads=[inv_b])

            w1v = od_w_in.rearrange("(c p) n -> p c n", p=128)
            wqv = od_wqb.rearrange("(c p) n -> p c n", p=128)
            wkvv = od_wkvb.rearrange("(c p) n -> p c n", p=128)
            SCALE = 192.0 ** -0.5

            A.off = mark_x1
            xt4 = [A.alloc("xt4b_%d" % i, [D], F32) for i in range(6)]
            mark_xa1 = A.off

            def preload_next_A1():
                for j in range(4):
                    dma_in("sp", xt4[2 + j], x1_d[(TTH + j) * 128:(TTH + j + 1) * 128, :], reads=[x1_tiles[TTH + j]])

            for half in range(2):
                dma_in("pool", wslab[0], w1v[:, :, 0:512])
                dma_in("pool", wslab[1], w1v[:, :, 512:1024])
                S.barrier()
                A.off = mark_xa1
                XA = {"xt": (xt4[0:4] if half == 0 else xt4[2:6]),
                      "hb": [A.alloc("hbA%d" % i, [D], BF16) for i in range(2)],
                      "junk": A.alloc("junkA", [D], BF16),
                      "ss8": A.alloc("ss8A", [TTH], F32)}
                S.mark("L1h%d:A" % half)
                phaseA(1, half, XA, preloaded=(4 if half == 1 else 0))
                S.mark("L1h%d:B0" % half)
                if dbg == "L1:%d:A" % half:
                    S.stopped = True
                if os.environ.get("KLIMIT"):
                    print("L1 B0 start nrec", S.nrec)
                S.barrier()
                A.off = mark_x1
                f2 = lambda n: A.alloc(n, [512], F32)
                rkvcol = A.alloc("rkvcol", [TTH], F32)
                CRq = A.alloc("CRq", [TH], F32, parts=64)
                SRq = A.alloc("SRq", [TH], F32, parts=64)
                tA = A.alloc("tA", [512], F32, parts=64)
                tB = A.alloc("tB", [512], F32, parts=64)
                wq = [A.alloc("wq%d" % i, [4, 256], BF16) for i in range(2)]
                wkv = [A.alloc("wkv%d" % i, [4, 256], BF16) for i in range(2)]
                mark_b1 = A.off

                def load_w(h):
                    if h >= 16:
                        return
                    if h % 4 == 0:
                        hg = h // 4
                        dma_in("pool", wslab[hg % 2], w1v[:, :, 1088 + hg * 512:1088 + (hg + 1) * 512])
                    w_ = wq[h % 2]
                    dma_in("pool", w_[:, :, 0:192], wqv[:, :, h * 192:(h + 1) * 192])
                    dma_in("pool", w_[:, :, 192:224], wqv[:, :, h * 192 + 160:h * 192 + 192])
                    dma_in("pool", w_[:, :, 224:256], wqv[:, :, h * 192 + 128:h * 192 + 160])
                    dma_in("pool", wkv[h % 2], wkvv[:, :, h * 256:(h + 1) * 256])

                wkpe = A.alloc("wkpe", [16, 128], BF16)
                sqA = [A.alloc("sqA%d" % i, [512], BF16) for i in range(2)]
                sdl = f2("sdl")
                posi = A.alloc("posi", [TH], I32, parts=64)
                uu = A.alloc("uu", [TH], F32, parts=64)
                ki = A.alloc("ki", [TH], I32, parts=64)
                kf = A.alloc("kf", [TH], F32, parts=64)

                dma_in("sp", posi, pos_d[half * TH:(half + 1) * TH].partition_broadcast(64))
                cp("dve", uu, posi)
                ts("dve", uu, uu, invcol[:, 0:1], None, ALU.mult)
                for tab, shift in ((Stab, 0.0), (Ctab, 0.25)):
                    if shift:
                        ts("dve", kf, uu, shift, None, ALU.add)
                        src_u = kf
                    else:
                        src_u = uu
                    cp("dve", ki, src_u)
                    cp("dve", tab, ki)
                    tt("dve", tab, src_u, tab, ALU.subtract)
                    ts("dve", kf, tab, 0.5, None, ALU.is_gt)
                    tt("dve", tab, tab, kf, ALU.subtract)
                    act(tab, tab, AF.Sin, scale=6.283185)
                ts("dve", Stab[0:32, :], Stab[0:32, :], -1.0, None, ALU.mult)

                if os.environ.get("KLIMIT"):
                    print("L1 B0 tables done nrec", S.nrec)
                dma_in("pool", wkpe[:, :, 0:64], w1v[:, :, 1024:1088])
                dma_in("pool", wkpe[:, :, 64:96], w1v[:, :, 1056:1088])
                dma_in("pool", wkpe[:, :, 96:128], w1v[:, :, 1024:1056])
                for blk in range(NBH):
                    tok = slice(blk * 512, (blk + 1) * 512)
                    gtok = slice(half * TH + blk * 512, half * TH + (blk + 1) * 512)
                    for (wsl, dstT, rst, ncol0) in ((wslab[0], cqsT, rstdq, C_QN), (wslab[1], ckvsT, rstdkv, C_KVN)):
                        for g in range(4):
                            bank = nextbank()
                            inproj(bank, wsl, g * 128, blk)
                            act(sqA[g % 2], bank, AF.Square)
                            act(dstT[:, g, tok], bank, AF.Identity, scale=cols[:, ncol0 + g:ncol0 + g + 1])
                            mm(pb[7], onesb, sqA[g % 2], start=(g == 0), stop=(g == 3))
                        act(sdl, pb[7], AF.Ln, bias=EPS, scale=1.0 / 512)
                        act(rst[:, tok], sdl, AF.Exp, scale=-0.5)
                    for tl in range(4):
                        tr(pb[6][:, tl * 128:(tl + 1) * 128], rstdkv[:, blk * 512 + tl * 128:blk * 512 + (tl + 1) * 128], identf)
                    cp("dve", rkvcol[:, blk * 4:(blk + 1) * 4], pb[6].re("p (j d) -> p j d", j=4)[:, :, 0])
                    bka = nextbank()
                    inproj(bka[0:64, :], wkpe, 0, blk, m=64)
                    bkb = nextbank()
                    inproj(bkb[0:64, :], wkpe, 64, blk, m=64)
                    tt("dve", tA, bka[0:64, :], Ctab[:, tok], ALU.mult)
                    tt("dve", tB, bkb[0:64, :], Stab[:, tok], ALU.mult)
                    tt("dve", kpeT[:, gtok], tA, tB, ALU.add)
                    tt("dve", CRq[:, tok], Ctab[:, tok], rstdq[0:64, tok], ALU.mult)
                    tt("dve", SRq[:, tok], Stab[:, tok], rstdq[0:64, tok], ALU.mult)

                load_w(0)
                if dbg == "L1:%d:B0" % half:
                    S.stopped = True
                S.barrier()
                A.off = mark_b1
                qnT = [A.alloc("qnT%d" % i, [TH], BF16) for i in range(2)]
                qrT = [A.alloc("qrT%d" % i, [TH], BF16, parts=64) for i in range(2)]
                knT = [A.alloc("knT%d" % i, [TH], BF16) for i in range(2)]
                Vtok = [A.alloc("Vtok%d" % i, [TTH, 128], BF16) for i in range(2)]
                knP = [A.alloc("knP%d" % i, [TH], BF16) for i in range(2)]
                VP = [A.alloc("VP%d" % i, [TTH, 128], BF16) for i in range(2)]
                sgate = [A.alloc("sgate%d" % i, [NBH, 512], F32) for i in range(2)]
                Eb = [A.alloc("Eb%d" % i, [512], BF16) for i in range(4)]
                rden = f2("rden")
                pbanks = [pb[0], pb[1], pb[2]]
                sbanks = [pb[3], pb[4], pb[7]]
                prot = [0]

                def pbank():
                    b = pbanks[prot[0] % 3]
                    prot[0] += 1
                    return b

                def proj(h):
                    p = h % 2
                    hg, hh = h // 4, h % 4
                    gslab = wslab[hg % 2]
                    load_w(h + 1)
                    if half == 1:
                        dma_in("sp", knP[p], kvs_d[h, 0], reads=[kvs_bufs[h]])
                        dma_in("sp", VP[p].re("p a b -> p (a b)"), kvs_d[h, 1], reads=[kvs_bufs[h]])
                    for blk in range(NBH):
                        tok = slice(blk * 512, (blk + 1) * 512)
                        bank = pbank()
                        for rc in range(4):
                            mm(bank, wq[p][:, rc, 0:128], cqsT[:, rc, tok], start=(rc == 0), stop=(rc == 3))
                        tt("dve", qnT[p][:, tok], bank, rstdq[:, tok], ALU.mult)
                        bka = pbank()
                        for rc in range(4):
                            mm(bka[0:64, :], wq[p][:, rc, 128:192], cqsT[:, rc, tok], start=(rc == 0), stop=(rc == 3))
                        bkb = pbank()
                        for rc in range(4):
                            mm(bkb[0:64, :], wq[p][:, rc, 192:256], cqsT[:, rc, tok], start=(rc == 0), stop=(rc == 3))
                        tt("dve", tA, bka[0:64, :], CRq[:, tok], ALU.mult)
                        tt("dve", tB, bkb[0:64, :], SRq[:, tok], ALU.mult)
                        tt("pool", qrT[p][:, tok], tA, tB, ALU.add)
                        bank = pbank()
                        for rc in range(4):
                            mm(bank, wkv[p][:, rc, 0:128], ckvsT[:, rc, tok], start=(rc == 0), stop=(rc == 3))
                        tt("dve", knT[p][:, tok], bank, rstdkv[:, tok], ALU.mult)
                        bank = pbank()
                        bv = bank.re("p (j d) -> p j d", j=4)
                        for tl in range(4):
                            t0 = blk * 512 + tl * 128
                            for rc in range(4):
                                mm(bv[:, tl, :], ckvsT[:, rc, t0:t0 + 128], wkv[p][:, rc, 128:256],
                                   start=(rc == 0), stop=(rc == 3))
                        for tl in range(4):
                            ti = blk * 4 + tl
                            act(Vtok[p][:, ti, :], bv[:, tl, :], AF.Copy, scale=rkvcol[:, ti:ti + 1])
                        bank = pbank()
                        inproj(bank, gslab, hh * 128, blk)
                        act(sgate[p][:, blk, :], bank, AF.Silu)
                    if half == 0:
                        dma_out("sp", kvs_d[h, 0], knT[p], writes=[kvs_bufs[h]])
                        dma_out("sp", kvs_d[h, 1], Vtok[p].re("p a b -> p (a b)"), writes=[kvs_bufs[h]])

                def attn(h):
                    p = h % 2
                    for qb in range(NBH):
                        qtok0 = qb * 512
                        tiles = []
                        if half == 1:
                            for kt in range(TTH):
                                tiles.append(("p", kt, 0))
                        for kt in range(4 * qb + 4):
                            r = kt - 4 * qb
                            tiles.append(("l", kt, max(r, -1)))
                        po = pb[5]
                        pd = pb[6]
                        nt = len(tiles)
                        Es = [None] * nt

                        def emit_s(i):
                            kind, kt, r = tiles[i]
                            c0 = 128 * r if r > 0 else 0
                            ps = sbanks[i % 3]
                            if kind == "p":
                                kn_src, kg = knP[p], kt
                            else:
                                kn_src, kg = knT[p], half * TTH + kt
                            mm(ps[:, c0:512], kn_src[:, kt * 128:(kt + 1) * 128], qnT[p][:, qtok0 + c0:qtok0 + 512], start=True, stop=False)
                            mm(ps[:, c0:512], kpeT[:, kg * 128:(kg + 1) * 128], qrT[p][:, qtok0 + c0:qtok0 + 512], start=False, stop=True)
                            E = Eb[i % 4]
                            act(E[:, c0:512], ps[:, c0:512], AF.Exp, scale=SCALE)
                            if kind == "l" and r >= 0:
                                tt("pool", E[:, c0:c0 + 128], E[:, c0:c0 + 128], tri128b, ALU.mult)
                            Es[i] = (E, c0)

                        def emit_pv(i):
                            kind, kt, r = tiles[i]
                            E, c0 = Es[i]
                            v_src = VP[p] if kind == "p" else Vtok[p]
                            mm(po[:, c0:512], v_src[:, kt, :], E[:, c0:512], start=(i == 0), stop=(i == nt - 1))
                            mm(pd[:, c0:512], onesb, E[:, c0:512], start=(i == 0), stop=(i == nt - 1))

                        emit_s(0)
                        if nt > 1:
                            emit_s(1)
                        for i in range(nt):
                            if i + 2 < nt:
                                emit_s(i + 2)
                            emit_pv(i)
                        act(rden, pd, AF.Ln)
                        act(rden, rden, AF.Exp, scale=-1.0)
                        tt("dve", rden, po, rden, ALU.mult)
                        tt("dve", V(outT_b[h][qb], outT.ap[:, h, qtok0:qtok0 + 512]), rden, sgate[p][:, qb, :], ALU.mult)

                S.mark("L1h%d:B1" % half)
                proj(0)
                for h in range(16):
                    if h + 1 < 16:
                        proj(h + 1)
                    attn(h)

                if dbg == "L1B" or dbg == "L1:%d:B1" % half:
                    S.stopped = True
                phaseC_preload(od_w_out)
                S.barrier()
                A.off = mark_xa1
                XC = {"yacc": None,
                      "xt": xt4,
                      "ss8": A.alloc("ss8C", [TTH], F32),
                      "ssq": A.alloc("ssqC", [TTH, 4], F32),
                      "junk": A.alloc("junkC", [512], BF16)}
                ytiles = []
                for i in range(TTH):
                    if i < TTH // 2:
                        ytiles.append(Buf(A.t[:, hT.off + i * D:hT.off + (i + 1) * D], "yaccL%d" % i))
                    else:
                        j = i - TTH // 2
                        ytiles.append(Buf(A.t[:, lat_off + j * D:lat_off + (j + 1) * D], "yaccU%d" % i))
                XC["yacc"] = _Y(ytiles)
                S.mark("L1h%d:C" % half)
                phaseC(1, half, XC, od_w_out, out_d, is_final=True,
                       after=(preload_next_A1 if half == 0 else None))
                if dbg == "L1:%d:C" % half:
                    S.stopped = True

        S.mark("END")
        S.emit(nc, st)
    nc._marks = S.marks
    return nc


_PROG = {}


def _get_prog(layers):
    if layers not in _PROG:
        _PROG[layers] = build_program(layers)
    return _PROG[layers]


def _in_maps_l0(inputs, xs):
    maps = []
    for c in range(8):
        b = c % 4
        maps.append({
            "x": np.ascontiguousarray(xs[b]),
            "norm_pre": np.ascontiguousarray(inputs["norm_pre"]),
            "norm_post": np.ascontiguousarray(inputs["norm_post"]),
            "ev_w_in": np.ascontiguousarray(inputs["ev_w_in"][0]),
            "ev_lb_logits": np.ascontiguousarray(inputs["ev_lb_logits"].reshape(16, 128)),
            "ev_a_onorm": np.ascontiguousarray(inputs["ev_a_onorm"].reshape(1, 128)),
            "ev_b_ln_w": np.ascontiguousarray(inputs["ev_b_ln_w"].reshape(8, 128)),
            "ev_b_ln_b": np.ascontiguousarray(inputs["ev_b_ln_b"].reshape(8, 128)),
            "ev_b_ws": np.ascontiguousarray(inputs["ev_b_ws"][0]),
            "ev_b_bias": np.ascontiguousarray(inputs["ev_b_bias"].reshape(1024)),
            "ev_w_out": np.ascontiguousarray(inputs["ev_w_out"][0]),
        })
    return maps


def _in_maps_l1(inputs, xs):
    maps = []
    for c in range(8):
        b = c % 4
        maps.append({
            "x": np.ascontiguousarray(xs[b]),
            "positions": np.ascontiguousarray(inputs["positions"][b]).astype(np.int32),
            "norm_pre": np.ascontiguousarray(inputs["norm_pre"]),
            "norm_post": np.ascontiguousarray(inputs["norm_post"]),
            "od_w_in": np.ascontiguousarray(inputs["od_w_in"][0]),
            "od_q_norm": np.ascontiguousarray(inputs["od_q_norm"].reshape(4, 128)),
            "od_w_qb": np.ascontiguousarray(inputs["od_w_qb"][0]),
            "od_kv_norm": np.ascontiguousarray(inputs["od_kv_norm"].reshape(4, 128)),
            "od_w_kvb": np.ascontiguousarray(inputs["od_w_kvb"][0]),
            "od_w_out": np.ascontiguousarray(inputs["od_w_out"][0]),
        })
    return maps


FUSED = True


def kernel(**inputs):
    inputs = {k: np.asarray(v) for k, v in inputs.items()}
    if FUSED:
        nc = _get_prog((0, 1))
        m0 = _in_maps_l0(inputs, inputs["x"])
        m1 = _in_maps_l1(inputs, inputs["x"])
        maps = [dict(a, **b) for a, b in zip(m1, m0)]
        res = run_bass_kernel_spmd(nc, maps, core_ids=list(range(8)))
        return np.stack([res.results[b]["out"] for b in range(4)]).astype(np.float32)
    nc0 = _get_prog((0,))
    res0 = run_bass_kernel_spmd(nc0, _in_maps_l0(inputs, inputs["x"]), core_ids=list(range(8)))
    x1 = [res0.results[b]["out"] for b in range(4)]
    nc1 = _get_prog((1,))
    res1 = run_bass_kernel_spmd(nc1, _in_maps_l1(inputs, x1), core_ids=list(range(8)))
    return np.stack([res1.results[b]["out"] for b in range(4)]).astype(np.float32)


if __name__ == "__main__":
    import time
    t0 = time.time()
    nc = build_program((0,))
    print("built", time.time() - t0)
```

```python
import contextlib
import math
import numpy as np
import concourse.bass as bass
import concourse.mybir as mybir
from concourse.bass_utils import run_bass_kernel_spmd

F32 = mybir.dt.float32
BF16 = mybir.dt.bfloat16
I32 = mybir.dt.int32
AF = mybir.ActivationFunctionType
ALU = mybir.AluOpType
AX = mybir.AxisListType

ENGS = ("sp", "pe", "act", "dve", "pool")
D = 2048
TFULL = 2048
TH = 1024
NBH = TH // 512
TTH = TH // 128
EPS = 1e-6


class V:
    def __init__(self, buf, ap):
        self.buf = buf
        self.ap = ap

    def __getitem__(self, idx):
        return V(self.buf, self.ap[idx])

    def re(self, pat, **kw):
        return V(self.buf, self.ap.rearrange(pat, **kw))

    def bc(self, shape):
        return V(self.buf, self.ap.to_broadcast(shape))

    def cast(self, dt):
        return V(self.buf, self.ap.bitcast(dt))


class Buf(V):
    def __init__(self, ap, name=""):
        self.buf = self
        self.ap = ap
        self.name = name
        self.w = None
        self.r = {}


class Sched:
    def __init__(self, ring=12):
        self.ops = {e: [] for e in ENGS}
        self.count = {e: 0 for e in ENGS}
        self.seen = {e: {} for e in ENGS}
        self.rings = {"sp": [("dma", "sp", i) for i in range(ring)],
                      "pool": [("dma", "pool", i) for i in range(12)]}
        self.ring_pos = {q: 0 for q in self.rings}
        self.dma_tot = {}
        for q in self.rings:
            for k in self.rings[q]:
                self.dma_tot[k] = 0
        self.out_dmas = []
        self.epoch = {}
        self.epoch_seen = {e: True for e in ENGS}

    def mark(self, name):
        if not hasattr(self, "marks"):
            self.marks = []
        self.marks.append((name, dict(self.count)))

    def barrier(self):
        ep = {}
        for e in ENGS:
            if self.count[e] > 0:
                ep[("eng", e)] = self.count[e]
        for k, v in self.dma_tot.items():
            if v > 0:
                ep[k] = v
        self.epoch = ep
        self.epoch_seen = {e: False for e in ENGS}

    def _deps(self, eng, reads, writes):
        deps = []
        if not self.epoch_seen[eng]:
            self.epoch_seen[eng] = True
            deps.extend(self.epoch.items())
        for b in reads:
            if b.w is not None:
                deps.append(b.w)
        for b in writes:
            if b.w is not None:
                deps.append(b.w)
            deps.extend(b.r.values())
        return deps

    def _need(self, eng, deps):
        best = {}
        for k, v in deps:
            if k == ("eng", "pe") and eng == "pe":
                continue
            if best.get(k, 0) < v:
                best[k] = v
        waits = []
        for k, v in best.items():
            if self.seen[eng].get(k, 0) < v:
                self.seen[eng][k] = v
                waits.append((k, v))
        return waits

    stopped = False
    nrec = 0
    limit = 0

    def op(self, eng, fn, reads=(), writes=()):
        self.nrec += 1
        if self.stopped or (self.limit and self.nrec > self.limit):
            return None
        waits = self._need(eng, self._deps(eng, reads, writes))
        self.count[eng] += 1
        me = (("eng", eng), self.count[eng])
        self.ops[eng].append((waits, fn, me, 1))
        for b in reads:
            b.r[me[0]] = me
        for b in writes:
            b.w = me
            b.r = {}
        return me

    def dma(self, q, fn, reads=(), writes=(), is_output=False, force=False):
        self.nrec += 1
        if not force and (self.stopped or (self.limit and self.nrec > self.limit)):
            return None
        deps = self._deps(q, reads, writes)
        ring = self.rings[q]
        k = ring[self.ring_pos[q] % len(ring)]
        self.ring_pos[q] += 1
        if self.dma_tot[k] > 0:
            deps.append((k, self.dma_tot[k]))
        waits = self._need(q, deps)
        self.dma_tot[k] += 16
        me = (k, self.dma_tot[k])
        self.ops[q].append((waits, fn, me, 16))
        for b in reads:
            b.r[k] = me
        for b in writes:
            b.w = me
            b.r = {}
        if is_output:
            self.out_dmas.append(me)
        return me

    def emit(self, nc, stack):
        sems = {}
        for e in ENGS:
            sems[("eng", e)] = stack.enter_context(nc.semaphore("s_" + e))
        for q in self.rings:
            for k in self.rings[q][: min(len(self.rings[q]), self.ring_pos[q])]:
                sems[k] = stack.enter_context(nc.semaphore("d_%s_%d" % (k[1], k[2])))
        final_waits = {}
        for k, v in self.out_dmas:
            final_waits[k] = max(final_waits.get(k, 0), v)
        for k, v in self.dma_tot.items():
            if v > 0:
                final_waits[k] = max(final_waits.get(k, 0), v)
        for e in ENGS:
            if e != "sp" and self.count[e] > 0:
                final_waits[("eng", e)] = self.count[e]

        def replay(name, eng):
            for waits, fn, me, amt in self.ops[name]:
                for k, v in waits:
                    eng.wait_ge(sems[k], v)
                ins = fn(eng)
                ins.then_inc(sems[me[0]], amt)
            if name == "sp":
                for k, v in final_waits.items():
                    eng.wait_ge(sems[k], v)

        block = stack.enter_context(nc.Block())

        @block.sync
        def _(e):
            replay("sp", e)

        @block.tensor
        def _(e):
            replay("pe", e)

        @block.scalar
        def _(e):
            replay("act", e)

        @block.vector
        def _(e):
            replay("dve", e)

        @block.gpsimd
        def _(e):
            replay("pool", e)


class Arena:
    def __init__(self, nc, stack, kbytes):
        self.n = kbytes * 256
        self.t = stack.enter_context(nc.sbuf_tensor("arena", [128, self.n], F32))
        self.off = 0

    def alloc(self, name, shape, dt=F32, parts=128):
        nel = 1
        for s in shape:
            nel *= s
        nbytes = nel * (2 if dt == BF16 else 4)
        ncol = (nbytes + 3) // 4
        assert self.off + ncol <= self.n, "arena overflow at %s: %d + %d > %d" % (name, self.off, ncol, self.n)
        ap = self.t[0:parts, self.off:self.off + ncol]
        off0 = self.off
        self.off += ncol
        if dt != F32:
            ap = ap.bitcast(dt)
        if len(shape) == 2:
            ap = ap.rearrange("p (a b) -> p a b", a=shape[0])
        elif len(shape) == 3:
            ap = ap.rearrange("p (a b c) -> p a b c", a=shape[0], b=shape[1])
        b = Buf(ap, name)
        b.off = off0
        b.ncol = ncol
        return b


class _Stop(Exception):
    pass


class _Y:
    def __init__(self, ytiles):
        self.ytiles = ytiles

    def __getitem__(self, idx):
        p, i, n = idx
        return self.ytiles[i][:, n]


def build_program(layers=(0, 1), dbg=None):
    do0 = 0 in layers
    do1 = 1 in layers
    nc = bass.Bass("TRN2", target_bir_lowering=False)
    S = Sched()
    import os
    S.limit = int(os.environ.get("KLIMIT", "0"))

    def din(name, shape, dt=F32):
        return nc.dram_tensor(name, shape, dt, kind="ExternalInput").ap()

    x_d = din("x", [TFULL, D])
    out_d = nc.dram_tensor("out", [TFULL, D], F32, kind="ExternalOutput").ap()
    if do0 and do1:
        x1_d = nc.dram_tensor("x1s", [TFULL, D], F32).ap()
    elif do0:
        x1_d = out_d
    else:
        x1_d = x_d
    npre_d = din("norm_pre", [2, D])
    npost_d = din("norm_post", [2, D])
    if do0:
        ev_w_in = din("ev_w_in", [D, 7168])
        ev_lb = din("ev_lb_logits", [16, 128])
        ev_onorm = din("ev_a_onorm", [1, 128])
        ev_lnw = din("ev_b_ln_w", [8, 128])
        ev_lnb = din("ev_b_ln_b", [8, 128])
        ev_ws = din("ev_b_ws", [8, 128, 128])
        ev_bias = din("ev_b_bias", [1024])
        ev_w_out = din("ev_w_out", [D, D])
    if do1:
        pos_d = din("positions", [TFULL], I32)
        od_w_in = din("od_w_in", [D, 3136])
        od_qn = din("od_q_norm", [4, 128])
        od_wqb = din("od_w_qb", [512, 3072])
        od_kvn = din("od_kv_norm", [4, 128])
        od_wkvb = din("od_w_kvb", [512, 4096])
        od_w_out = din("od_w_out", [D, D])
        kvs_d = nc.dram_tensor("kvs", [16, 2, 128, TH], BF16).ap()

    dbg_d = nc.dram_tensor("dbg", [128, 16 * TH], BF16, kind="ExternalOutput").ap() if (dbg and not dbg.startswith("L1:")) else None
    x1_tiles = [Buf(None, "x1t%d" % i) for i in range(TFULL // 128)]
    kvs_bufs = [Buf(None, "kvs%d" % i) for i in range(16)]

    with contextlib.ExitStack() as st:
        A = Arena(nc, st, 207)
        pb = [Buf(st.enter_context(nc.psum_tensor("pb%d" % i, [128, 512], F32))[:, :], "pb%d" % i) for i in range(8)]

        def mm(out, lhsT, rhs, start=True, stop=True, reads=()):
            S.op("pe", lambda e: e.matmul(out.ap, lhsT=lhsT.ap, rhs=rhs.ap, start=start, stop=stop),
                 reads=[lhsT.buf, rhs.buf, *reads], writes=[out.buf])

        def tr(out, in_, ident):
            S.op("pe", lambda e: e.transpose(out=out.ap, in_=in_.ap, identity=ident.ap),
                 reads=[in_.buf, ident.buf], writes=[out.buf])

        def act(out, in_, func, bias=None, scale=None, accum=None, reads=(), junk_out=False):
            kw = {}
            rd = [in_.buf, *reads]
            wr = [] if junk_out else [out.buf]
            if bias is not None:
                if isinstance(bias, V):
                    kw["bias"] = bias.ap
                    rd.append(bias.buf)
                else:
                    kw["bias"] = float(bias)
            if scale is not None:
                if isinstance(scale, V):
                    kw["scale"] = scale.ap
                    rd.append(scale.buf)
                else:
                    kw["scale"] = float(scale)
            if accum is not None:
                kw["accum_out"] = accum.ap
                wr.append(accum.buf)
            S.op("act", lambda e: e.activation(out=out.ap, in_=in_.ap, func=func, **kw), reads=rd, writes=wr)

        def tt(eng, out, in0, in1, op):
            S.op(eng, lambda e: e.tensor_tensor(out=out.ap, in0=in0.ap, in1=in1.ap, op=op),
                 reads=[in0.buf, in1.buf], writes=[out.buf])

        def ts(eng, out, in0, s1, s2, op0, op1=None):
            rd = [in0.buf]
            a1 = s1
            a2 = s2
            if isinstance(s1, V):
                rd.append(s1.buf)
                a1 = s1.ap
            if isinstance(s2, V):
                rd.append(s2.buf)
                a2 = s2.ap
            if op1 is None:
                S.op(eng, lambda e: e.tensor_scalar(out=out.ap, in0=in0.ap, scalar1=a1, scalar2=None, op0=op0),
                     reads=rd, writes=[out.buf])
            else:
                S.op(eng, lambda e: e.tensor_scalar(out=out.ap, in0=in0.ap, scalar1=a1, scalar2=a2, op0=op0, op1=op1),
                     reads=rd, writes=[out.buf])

        def stt(out, in0, scalar, in1, op0, op1):
            rd = [in0.buf, in1.buf]
            a = scalar
            if isinstance(scalar, V):
                rd.append(scalar.buf)
                a = scalar.ap
            S.op("dve", lambda e: e.scalar_tensor_tensor(out=out.ap, in0=in0.ap, scalar=a, in1=in1.ap, op0=op0, op1=op1),
                 reads=rd, writes=[out.buf])

        def cp(eng, out, in_):
            if eng == "act":
                S.op("act", lambda e: e.activation(out=out.ap, in_=in_.ap, func=AF.Copy), reads=[in_.buf], writes=[out.buf])
            else:
                S.op(eng, lambda e: e.tensor_copy(out=out.ap, in_=in_.ap), reads=[in_.buf], writes=[out.buf])

        def recip(out, in_):
            S.op("dve", lambda e: e.reciprocal(out=out.ap, in_=in_.ap), reads=[in_.buf], writes=[out.buf])

        def red(out, in_):
            S.op("dve", lambda e: e.tensor_reduce(out=out.ap, in_=in_.ap, axis=AX.X, op=ALU.add),
                 reads=[in_.buf], writes=[out.buf])

        def memset(eng, out, val):
            S.op(eng, lambda e: e.memset(out.ap, val), writes=[out.buf])

        def dma_in(q, out, src_ap, reads=()):
            S.dma(q, lambda e: e.dma_start(out=out.ap, in_=src_ap), reads=list(reads), writes=[out.buf])

        def dma_out(q, dst_ap, in_, writes=(), is_output=False):
            S.dma(q, lambda e: e.dma_start(out=dst_ap, in_=in_.ap), reads=[in_.buf], writes=list(writes), is_output=is_output)

        identf = A.alloc("identf", [128], F32)
        identb = A.alloc("identb", [128], BF16)
        tri64 = A.alloc("tri64", [64], F32)
        tri128b = A.alloc("tri128b", [128], BF16)
        wsmask = A.alloc("wsmask", [128], F32)
        ones1 = A.alloc("ones1", [128], F32)
        onesb = A.alloc("onesb", [128], BF16)
        cmask = A.alloc("cmask", [512], BF16)
        pst = A.alloc("pst", [128], F32)
        cols = A.alloc("cols", [48], F32)
        tmpc = A.alloc("tmpc", [128], F32)

        memset("pool", identf, 1.0)
        S.op("pool", lambda e: e.affine_select(out=identf.ap, in_=identf.ap, pattern=[[-1, 128]], compare_op=ALU.is_equal,
                                               fill=0.0, base=0, channel_multiplier=1), reads=[identf], writes=[identf])
        cp("dve", identb, identf)
        memset("pool", tri64, 1.0)
        S.op("pool", lambda e: e.affine_select(out=tri64[0:64, :].ap, in_=tri64[0:64, :].ap, pattern=[[1, 64]],
                                               compare_op=ALU.is_ge, fill=0.0, base=0, channel_multiplier=-1),
             reads=[tri64], writes=[tri64])
        memset("pool", tmpc, 1.0)
        S.op("pool", lambda e: e.affine_select(out=tmpc.ap, in_=tmpc.ap, pattern=[[1, 128]], compare_op=ALU.is_ge,
                                               fill=0.0, base=0, channel_multiplier=-1), reads=[tmpc], writes=[tmpc])
        cp("dve", tri128b, tmpc)
        memset("pool", wsmask, 1.0)
        S.op("pool", lambda e: e.affine_select(out=wsmask.ap, in_=wsmask.ap, pattern=[[-1, 128]], compare_op=ALU.is_ge,
                                               fill=0.0, base=0, channel_multiplier=1), reads=[wsmask], writes=[wsmask])
        memset("pool", ones1, 1.0)
        memset("pool", onesb, 1.0)
        memset("pool", cmask, 1.0)
        memset("pool", cmask.re("p (j t) -> p j t", j=8)[:, :, 0:1], 0.0)
        memset("pool", pst, 0.0)

        if do0:
            dma_in("sp", pst[0:16, :], ev_lb)
            dma_in("sp", pst[16:17, :], ev_onorm)
            dma_in("sp", pst[17:25, :], ev_lnw)
            dma_in("sp", pst[25:33, :], ev_lnb)
        if do1:
            dma_in("sp", pst[33:37, :], od_qn)
            dma_in("sp", pst[37:41, :], od_kvn)
        tr(pb[0][:, 0:48], pst[0:48, :], identf[0:48, 0:48])
        cp("dve", cols, pb[0][:, 0:48])
        C_LB, C_ON, C_LNW, C_LNB, C_QN, C_KVN = 0, 16, 17, 25, 33, 37

        wbc = A.alloc("wbc", [D], F32)
        hT = A.alloc("hT", [16, TH], BF16)
        hT_tiles = [Buf(hT.ap, "hT%d" % i) for i in range(TTH)]
        outT = A.alloc("outT", [16, TH], BF16)
        outT_b = [[Buf(outT.ap, "outT%d_%d" % (c, b)) for b in range(NBH)] for c in range(16)]
        wslab = [A.alloc("wslab%d" % i, [16, 512], BF16) for i in range(2)]
        mark_layer = A.off
        if dbg and not dbg.startswith("L1:"):
            memset("pool", outT, 0.0)

        bank_rot = [0]

        def nextbank():
            b = pb[bank_rot[0] % 3]
            bank_rot[0] += 1
            return b

        def inproj(dst, w, col0, blk, m=128):
            tok = slice(blk * 512, (blk + 1) * 512)
            tiles = hT_tiles[4 * blk:4 * blk + 4]
            for c in range(16):
                mm(dst, w[:, c, col0:col0 + m], V(tiles[0], hT.ap[:, c, tok]), start=(c == 0), stop=(c == 15),
                   reads=tiles[1:])

        def phaseA(layer, half, XA, preloaded=0, hook=None):
            src = x_d if layer == 0 else x1_d
            dma_in("sp", wbc, npre_d[layer, :].partition_broadcast(128))
            ss8 = XA["ss8"]
            junk = XA["junk"]
            k = 0
            def tile_out(i, j):
                xt = XA["xt"][j]
                hb = XA["hb"][i % 2]
                stt(hb, xt, ss8[:, i:i + 1], wbc, ALU.mult, ALU.mult)
                tpa = pb[(2 * kk_[0]) % 8].cast(BF16).re("p (c d) -> p c d", c=8)
                tpb = pb[(2 * kk_[0] + 1) % 8].cast(BF16).re("p (c d) -> p c d", c=8)
                kk_[0] += 1
                for c in range(16):
                    dst = tpa if c < 8 else tpb
                    tr(dst[:, c % 8, :], hb[:, c * 128:(c + 1) * 128], identb)
                tl = hT_tiles[i]
                cp("act", V(tl, hT.ap[:, 0:8, i * 128:(i + 1) * 128]), tpa)
                cp("dve", V(tl, hT.ap[:, 8:16, i * 128:(i + 1) * 128]), tpb)
                if hook is not None:
                    hook(i)

            def rstd_of(sl):
                ts("dve", sl, sl, 1.0 / D, EPS, ALU.mult, ALU.add)
                act(sl, sl, AF.Sqrt)
                recip(sl, sl)

            kk_ = [0]
            for wv_ in range(TTH // 4):
                for j in range(4):
                    i = wv_ * 4 + j
                    gi = half * TTH + i
                    xt = XA["xt"][j]
                    rd = [x1_tiles[gi]] if layer == 1 else []
                    if i >= preloaded:
                        dma_in("sp", xt, src[gi * 128:(gi + 1) * 128, :], reads=rd)
                    act(junk, xt, AF.Square, accum=ss8[:, i:i + 1], junk_out=True)
                    if i == 0:
                        rstd_of(ss8[:, 0:1])
                if wv_ == 0:
                    tile_out(0, 0)
                    rstd_of(ss8[:, 1:4])
                    first = 1
                else:
                    rstd_of(ss8[:, wv_ * 4:(wv_ + 1) * 4])
                    first = 0
                for j in range(first, 4):
                    tile_out(wv_ * 4 + j, j)

        def phaseC_preload(w_out_d):
            wv = w_out_d.rearrange("(c p) n -> p c n", p=128)
            for nb in range(2):
                dma_in("pool", wslab[nb % 2], wv[:, :, nb * 512:(nb + 1) * 512])

        def phaseC(layer, half, XC, w_out_d, dst_d, is_final, after=None):
            src = x_d if layer == 0 else x1_d
            dma_in("sp", wbc, npost_d[layer, :].partition_broadcast(128))
            yacc = XC["yacc"]
            ss8 = XC["ss8"]
            ssq = XC["ssq"]
            junk = XC["junk"]
            wv = w_out_d.rearrange("(c p) n -> p c n", p=128)
            nxt = len(XC["xt"])

            def load_x(i):
                gi = half * TTH + i
                rd = [x1_tiles[gi]] if layer == 1 else []
                dma_in("sp", XC["xt"][i % nxt], src[gi * 128:(gi + 1) * 128, :], reads=rd)

            for i in range(nxt):
                load_x(i)
            k = 0
            for nb in range(4):
                w = wslab[nb % 2]
                if nb >= 2:
                    dma_in("pool", w, wv[:, :, nb * 512:(nb + 1) * 512])
                for i in range(TTH):
                    bank = pb[k % 8]
                    k += 1
                    for c in range(16):
                        mm(bank, V(outT_b[c][i // 4], outT.ap[:, c, i * 128:(i + 1) * 128]), w[:, c, :],
                           start=(c == 0), stop=(c == 15))
                    act(junk, bank, AF.Square, accum=ssq[:, i, nb:nb + 1], junk_out=True)
                    yv = yacc[:, i, nb * 512:(nb + 1) * 512]
                    wv_ = wbc[:, nb * 512:(nb + 1) * 512]
                    S.op("dve", (lambda o, a, b: (lambda e: e.tensor_tensor(out=o, in0=a, in1=b, op=ALU.mult)))(yv.ap, bank.ap, wv_.ap),
                         reads=[bank, wbc, ssq], writes=[yv.buf])
            tt("dve", ss8, ssq[:, :, 0], ssq[:, :, 1], ALU.add)
            tt("dve", ss8, ss8, ssq[:, :, 2], ALU.add)
            tt("dve", ss8, ss8, ssq[:, :, 3], ALU.add)
            ts("dve", ss8, ss8, 1.0 / D, EPS, ALU.mult, ALU.add)
            act(ss8, ss8, AF.Sqrt)
            recip(ss8, ss8)
            for i in range(TTH):
                gi = half * TTH + i
                xt = XC["xt"][i % nxt]
                yt = yacc[:, i, :]
                stt(yt, yt, ss8[:, i:i + 1], xt, ALU.mult, ALU.add)
                wr = [x1_tiles[gi]] if (layer == 0) else []
                dma_out("sp", dst_d[gi * 128:(gi + 1) * 128, :], yt, writes=wr, is_output=is_final)
                if i + nxt < TTH:
                    load_x(i + nxt)
                if i == nxt - 1 and after is not None:
                    after()

        def dump(buf16):
            S.dma("sp", lambda e: e.dma_start(out=dbg_d, in_=buf16.ap.rearrange("p c t -> p (c t)")), reads=[buf16] + hT_tiles + [b for r in outT_b for b in r], is_output=True, force=True)
            S.stopped = True

        if do0:
            lbc = A.alloc("lbc", [8], F32)
            omlc = A.alloc("omlc", [8], F32)
            nomlc = A.alloc("nomlc", [8], F32)
            wsT = A.alloc("wsT", [8, 128], BF16)
            Cg = A.alloc("Cg", [8, 128], F32)
            biasbc = A.alloc("biasbc", [8, 128], F32)
            Sfin = A.alloc("Sfin", [8, 128], F32)
            wsraw = [A.alloc("wsraw%d" % i, [128], F32) for i in range(2)]
            wsm = [A.alloc("wsm%d" % i, [128], BF16) for i in range(2)]
            mark_x = A.off
            xt4 = [A.alloc("xt4_%d" % i, [D], F32) for i in range(6)]
            mark_xa = A.off
            for j in range(4):
                dma_in("sp", xt4[j], x_d[j * 128:(j + 1) * 128, :])

            tt("dve", lbc, cols[:, C_LB:C_LB + 8], cols[:, C_LB + 8:C_LB + 16], ALU.subtract)
            act(lbc, lbc, AF.Sigmoid)
            ts("dve", omlc, lbc, -1.0, 1.0, ALU.mult, ALU.add)
            ts("dve", nomlc, omlc, -1.0, None, ALU.mult)
            dma_in("sp", biasbc, ev_bias.partition_broadcast(128).rearrange("p (g t) -> p g t", g=8))
            def ws_setup(g):
                tt("dve", wsm[g % 2], wsraw[g % 2], wsmask, ALU.mult)
                if g + 2 < 8:
                    pass
                tpw = pb[1].cast(BF16)[:, 0:128]
                tr(tpw, wsm[g % 2], identb)
                cp("act", wsT[:, g, :], tpw)
                mm(pb[2][:, 0:128], onesb, wsT[:, g, :])
                stt(Cg[:, g, :], pb[2][:, 0:128], cols[:, C_LNB + g:C_LNB + g + 1], biasbc[:, g, :], ALU.mult, ALU.add)
                if g + 2 < 8:
                    dma_in("sp", wsraw[g % 2], ev_ws[g + 2])

            dma_in("sp", wsraw[0], ev_ws[0])
            dma_in("sp", wsraw[1], ev_ws[1])

            w0v = ev_w_in.rearrange("(c p) n -> p c n", p=128)

            def preload_next_A0():
                for j in range(4):
                    dma_in("sp", xt4[2 + j], x_d[(TTH + j) * 128:(TTH + j + 1) * 128, :])

            for half in range(2):
                slab_of = {}
                slab_cnt = [0]

                def load_slab(kind, idx):
                    key = (kind, idx)
                    if key in slab_of:
                        return
                    w = wslab[slab_cnt[0] % 2]
                    slab_cnt[0] += 1
                    slab_of[key] = w
                    if kind == "h":
                        for gi_, base in enumerate((1024, 0, 2048, 3072)):
                            c0 = base + idx * 128
                            dma_in("pool", w[:, :, gi_ * 128:(gi_ + 1) * 128], w0v[:, :, c0:c0 + 128])
                    else:
                        for gi_, base in enumerate((4096, 5120, 6144, 6144) if os.environ.get("G4") else (4096, 5120, 6144)):
                            c0 = base + idx * 128
                            dma_in("pool", w[:, :, gi_ * 128:(gi_ + 1) * 128], w0v[:, :, c0:c0 + 128])

                load_slab("h", 0)
                load_slab("h", 1)
                if half > 0:
                    S.barrier()
                A.off = mark_xa
                XA = {"xt": (xt4[0:4] if half == 0 else xt4[2:6]),
                      "hb": [A.alloc("hbA%d" % i, [D], BF16) for i in range(2)],
                      "junk": A.alloc("junkA", [D], BF16),
                      "ss8": A.alloc("ss8A", [TTH], F32)}
                S.mark("L0h%d:A" % half)
                phaseA(0, half, XA, preloaded=4, hook=(ws_setup if half == 0 else None))
                S.mark("L0h%d:B" % half)
                if dbg == "A":
                    dump(hT)
                S.barrier()
                A.off = mark_x
                f2 = lambda n: A.alloc(n, [512], F32)
                b2 = lambda n: A.alloc(n, [512], BF16)
                sig, logf, bb, kk, ek, dd = f2("sig"), f2("logf"), f2("bb"), f2("kk"), f2("ek"), f2("dd")
                ktil = [b2("ktil%d" % i) for i in range(2)]
                kdec = [b2("kdec%d" % i) for i in range(2)]
                vT = [b2("vT%d" % i) for i in range(2)]
                kdT = A.alloc("kdT", [8, 128], BF16, parts=64)
                eq = [f2("eq%d" % i) for i in range(2)]
                qtil = [b2("qtil%d" % i) for i in range(3)]
                sg = [f2("sg%d" % i) for i in range(3)]
                vtok = [A.alloc("vtok%d" % i, [8, 128], BF16, parts=64) for i in range(2)]
                scT = [A.alloc("scT%d" % i, [8, 64], BF16, parts=64) for i in range(2)]
                SbA = [A.alloc("SbA%d" % i, [8, 128], BF16) for i in range(3)]
                Sf = [A.alloc("Sf%d" % i, [128], F32) for i in range(2)]
                osqb, sd, t1 = b2("osqb"), f2("sd"), f2("t1")
                sqv = f2("sqv")
                vTf = f2("vTf")
                st4 = [A.alloc("st4_%d" % i, [4], F32) for i in range(6)]
                vn = A.alloc("vn", [4, 128], BF16)
                svs, sgb, usg = f2("svs"), f2("sgb"), f2("usg")

                units = []
                for hd in range(8):
                    for blk in range(NBH):
                        units.append(("h", hd, blk))
                for g in range(8):
                    for blk in range(NBH):
                        units.append(("g", g, blk))

                def next_slab_key(ui):
                    kind, idx, blk = units[ui]
                    for uj in range(ui + 1, len(units)):
                        if (units[uj][0], units[uj][1]) != (kind, idx):
                            return (units[uj][0], units[uj][1])
                    return None

                def front_a(ui):
                    _, hd, blk = units[ui]
                    par = ui % 2
                    p3 = ui % 3
                    gblk = half * NBH + blk
                    load_slab("h", hd)
                    if blk == 0:
                        nk = next_slab_key(ui)
                        if nk is not None:
                            load_slab(*nk)
                    w = slab_of[("h", hd)]
                    lb_c = lbc[:, hd:hd + 1]
                    oml_c = omlc[:, hd:hd + 1]
                    noml_c = nomlc[:, hd:hd + 1]
                    pf = nextbank()
                    inproj(pf, w, 0, blk)
                    act(sig, pf, AF.Sigmoid)
                    pg = nextbank()
                    inproj(pg, w, 384, blk)
                    act(sg[p3], pg, AF.Silu)
                    pq = nextbank()
                    inproj(pq, w, 128, blk)
                    act(logf, sig, AF.Ln, bias=lb_c, scale=oml_c)
                    ts("pool", kk, sig, noml_c, oml_c, ALU.mult, ALU.add)
                    S.op("dve", lambda e: e.tensor_tensor_scan(out=bb.ap, data0=cmask.ap, data1=logf.ap, initial=0.0,
                                                               op0=ALU.mult, op1=ALU.add),
                         reads=[cmask, logf], writes=[bb])
                    act(eq[par], bb, AF.Exp)
                    act(ek, bb, AF.Exp, scale=-1.0)
                    bb3 = bb.re("p (j t) -> p j t", j=8)
                    tt("dve", dd.re("p (j t) -> p j t", j=8), bb3, bb3[:, :, 63:64].bc([128, 8, 64]), ALU.subtract)
                    act(dd, dd, AF.Exp, scale=-1.0)
                    tt("dve", qtil[p3], pq, eq[par], ALU.mult)
                    pi_ = nextbank()
                    inproj(pi_, w, 256, blk)
                    tt("pool", ktil[par], kk, ek, ALU.mult)
                    tt("pool", kdec[par], kk, dd, ALU.mult)
                    cp("act", vT[par], pi_)

                def front_b(ui, mid=None):
                    _, hd, blk = units[ui]
                    par = ui % 2
                    p3 = ui % 3
                    gblk = half * NBH + blk
                    tpv = pb[3][0:64, :].cast(BF16).re("p (j d) -> p j d", j=8)
                    for j in range(8):
                        tr(tpv[:, j, :], vT[par][:, j * 64:(j + 1) * 64], identb)
                    cp("dve", vtok[par], tpv)
                    for j in range(8):
                        tr(tpv[:, j, :], kdec[par][:, j * 64:(j + 1) * 64], identb)
                    cp("dve", kdT, tpv)
                    psc = pb[4][0:64, :].re("p (j t) -> p j t", j=8)
                    for j in range(8):
                        mm(psc[:, j, :], ktil[par][:, j * 64:(j + 1) * 64], qtil[p3][:, j * 64:(j + 1) * 64])
                    tt("dve", scT[par], psc, tri64[0:64, :].re("p (o t) -> p o t", o=1).bc([64, 8, 64]), ALU.mult)
                    pkv = pb[5].re("p (j d) -> p j d", j=4)
                    last_unit_of_head = (blk == NBH - 1)
                    for hh in range(2):
                        for j4 in range(4):
                            j = hh * 4 + j4
                            mm(pkv[:, j4, :], kdT[:, j, :], vtok[par][:, j, :])
                        for j4 in range(4):
                            j = hh * 4 + j4
                            c = gblk * 8 + j
                            dcol = eq[par][:, j * 64 + 63:j * 64 + 64]
                            is_last = last_unit_of_head and j == 7
                            if is_last:
                                dst = Sfin[:, hd, :]
                            else:
                                dst = Sf[(c + 1) % 2]
                            if c == 0:
                                cp("dve", dst, pkv[:, j4, :])
                            else:
                                if j == 0 and blk == 0:
                                    prev = Sfin[:, hd, :]
                                else:
                                    prev = Sf[c % 2]
                                stt(dst, prev, dcol, pkv[:, j4, :], ALU.mult, ALU.add)
                            if not is_last:
                                if j < 7:
                                    cp("pool", SbA[ui % 3][:, j + 1, :], dst)
                                else:
                                    cp("pool", SbA[(ui + 1) % 3][:, 0, :], dst)
                        if hh == 0 and mid is not None:
                            mid()

                def pre_h(ui):
                    _, hd, blk = units[ui]
                    if half == 1 and blk == 0:
                        cp("pool", SbA[ui % 3][:, 0, :], Sfin[:, hd, :])

                def back_o(ui):
                    _, hd, blk = units[ui]
                    par = ui % 2
                    p3 = ui % 3
                    gblk = half * NBH + blk
                    po = pb[6]
                    for j in range(8):
                        c = gblk * 8 + j
                        first = (c == 0)
                        mm(po[:, j * 64:(j + 1) * 64], vtok[par][:, j, :], scT[par][:, j, :], start=True, stop=first)
                        if not first:
                            mm(po[:, j * 64:(j + 1) * 64], SbA[ui % 3][:, j, :], qtil[p3][:, j * 64:(j + 1) * 64],
                               start=False, stop=True)

                def back_n(ui):
                    _, hd, blk = units[ui]
                    par = ui % 2
                    p3 = ui % 3
                    po = pb[6]
                    act(osqb, po, AF.Square)
                    mm(pb[7], onesb, osqb)
                    act(sd, pb[7], AF.Ln, bias=EPS, scale=1.0 / 128)
                    act(sd, sd, AF.Exp, scale=-0.5)
                    stt(t1, po, cols[:, C_ON:C_ON + 1], sd, ALU.mult, ALU.mult)
                    tt("dve", V(outT_b[hd][blk], outT.ap[:, hd, blk * 512:(blk + 1) * 512]), t1, sg[p3], ALU.mult)

                def front_g(ui):
                    _, g, blk = units[ui]
                    if os.environ.get("KLIMIT"):
                        print("front_g start nrec", S.nrec)
                    load_slab("g", g)
                    if blk == 0 and not os.environ.get("NOPF"):
                        nk = next_slab_key(ui)
                        if nk is not None:
                            load_slab(*nk)
                    w = slab_of[("g", g)]
                    pvb = nextbank()
                    pv = pvb.re("p (j d) -> p j d", j=4)
                    for tl in range(4):
                        tile = hT_tiles[4 * blk + tl]
                        t0 = blk * 512 + tl * 128
                        for c in range(16):
                            mm(pv[:, tl, :], V(tile, hT.ap[:, c, t0:t0 + 128]), w[:, c, 128:256], start=(c == 0), stop=(c == 15))
                    s1, s2, mean, msq, rstd, nmr = st4
                    for tl in range(4):
                        act(vTf[:, tl * 128:(tl + 1) * 128], pv[:, tl, :], AF.Identity, accum=s1[:, tl:tl + 1], junk_out=True)
                        act(sqv[:, tl * 128:(tl + 1) * 128], pv[:, tl, :], AF.Square, accum=s2[:, tl:tl + 1], junk_out=True)
                    pu = nextbank()
                    inproj(pu, w, 0, blk)
                    ts("dve", mean, s1, 1.0 / 128, None, ALU.mult)
                    tt("dve", msq, mean, mean, ALU.mult)
                    stt(msq, s2, 1.0 / 128, msq, ALU.mult, ALU.subtract)
                    act(rstd, msq, AF.Sqrt, bias=EPS)
                    recip(rstd, rstd)
                    stt(nmr, mean, -1.0, rstd, ALU.mult, ALU.mult)
                    for tl in range(4):
                        act(vn[:, tl, :], pv[:, tl, :], AF.Identity, bias=nmr[:, tl:tl + 1], scale=rstd[:, tl:tl + 1])
                    pgt = nextbank()
                    inproj(pgt, w, 256, blk)
                    act(sgb, pgt, AF.Silu)
                    tt("dve", usg, pu, sgb, ALU.mult)
                    psv = pb[4]
                    for tl in range(4):
                        mm(psv[:, tl * 128:(tl + 1) * 128], vn[:, tl, :], wsT[:, g, :])
                    stt(svs.re("p (j t) -> p j t", j=4), psv.re("p (j t) -> p j t", j=4), cols[:, C_LNW + g:C_LNW + g + 1],
                        Cg[:, g, :].re("p (o t) -> p o t", o=1).bc([128, 4, 128]), ALU.mult, ALU.add)
                    tt("dve", V(outT_b[8 + g][blk], outT.ap[:, 8 + g, blk * 512:(blk + 1) * 512]), svs, usg, ALU.mult)

                n_u = len(units)
                if dbg == "B1":
                    units = units[:2] + units[16:18]
                    n_u = len(units)
                if dbg and dbg.startswith("B:"):
                    lo, hi = dbg[2:].split("-")
                    units = units[int(lo):int(hi)]
                    n_u = len(units)
                if dbg and dbg.startswith("BL:"):
                    units = [units[int(t)] for t in dbg[3:].split(",")]
                    n_u = len(units)
                for ui in range(n_u + 2):
                    if ui < n_u:
                        if units[ui][0] == "h":
                            front_a(ui)
                        else:
                            front_g(ui)
                    has_c = (2 <= ui and units[ui - 2][0] == "h")
                    if 1 <= ui <= n_u and units[ui - 1][0] == "h":
                        pre_h(ui - 1)
                        front_b(ui - 1, mid=((lambda u=ui - 2: back_o(u)) if has_c else None))
                        if has_c:
                            back_n(ui - 2)
                    elif has_c:
                        back_o(ui - 2)
                        back_n(ui - 2)

                if dbg and dbg[0] == "B":
                    dump(outT)
                phaseC_preload(ev_w_out)
                S.barrier()
                A.off = mark_xa
                XC = {"yacc": None,
                      "xt": xt4,
                      "ss8": A.alloc("ss8C", [TTH], F32),
                      "ssq": A.alloc("ssqC", [TTH, 4], F32),
                      "junk": A.alloc("junkC", [512], BF16)}
                yup = A.alloc("yaccU", [TTH // 2, D], F32)
                ytiles = []
                for i in range(TTH):
                    if i < TTH // 2:
                        ytiles.append(Buf(A.t[:, hT.off + i * D:hT.off + (i + 1) * D], "yaccL%d" % i))
                    else:
                        ytiles.append(Buf(yup.ap[:, i - TTH // 2, :], "yaccU%d" % i))
                XC["yacc"] = _Y(ytiles)
                S.mark("L0h%d:C" % half)
                if os.environ.get("KLIMIT"):
                    print("L0 C start nrec", S.nrec)
                phaseC(0, half, XC, ev_w_out, x1_d, is_final=(not do1),
                       after=(preload_next_A0 if half == 0 else None))
            A.off = mark_layer

        if do1 and not S.stopped:
            A.off = mark_layer
            kpeT = A.alloc("kpeT", [TFULL], BF16, parts=64)
            nc._kpeT_off = kpeT.off
            invrow = A.alloc("invrow", [64], F32, parts=1)
            invcol = A.alloc("invcol", [1], F32, parts=64)
            lat_off = A.off
            cqsT = A.alloc("cqsT", [4, TH], BF16)
            ckvsT = A.alloc("ckvsT", [4, TH], BF16)
            rstdq = A.alloc("rstdq", [TH], F32)
            rstdkv = A.alloc("rstdkv", [TH], F32)
            Ctab = A.alloc("Ctab", [TH], F32, parts=64)
            Stab = A.alloc("Stab", [TH], F32, parts=64)
            assert A.off - lat_off == 8192
            mark_x1 = A.off

            for p in range(64):
                val = float(np.float32(10000.0 ** (-(p % 32) / 32.0))) / (2.0 * math.pi)
                memset("pool", invrow[0:1, p:p + 1], val)
            inv_d = nc.dram_tensor("invs", [64], F32).ap()
            inv_b = Buf(None, "invs")
            dma_out("sp", inv_d.rearrange("(o n) -> o n", o=1), invrow, writes=[inv_b])
            dma_in("sp", invcol, inv_d.rearrange("(p x) -> p x", x=1), reads=[inv_b])

            w1v = od_w_in.rearrange("(c p) n -> p c n", p=128)
            wqv = od_wqb.rearrange("(c p) n -> p c n", p=128)
            wkvv = od_wkvb.rearrange("(c p) n -> p c n", p=128)
            SCALE = 192.0 ** -0.5

            A.off = mark_x1
            xt4 = [A.alloc("xt4b_%d" % i, [D], F32) for i in range(6)]
            mark_xa1 = A.off

            def preload_next_A1():
                for j in range(4):
                    dma_in("sp", xt4[2 + j], x1_d[(TTH + j) * 128:(TTH + j + 1) * 128, :], reads=[x1_tiles[TTH + j]])

            for half in range(2):
                dma_in("pool", wslab[0], w1v[:, :, 0:512])
                dma_in("pool", wslab[1], w1v[:, :, 512:1024])
                S.barrier()
                A.off = mark_xa1
                XA = {"xt": (xt4[0:4] if half == 0 else xt4[2:6]),
                      "hb": [A.alloc("hbA%d" % i, [D], BF16) for i in range(2)],
                      "junk": A.alloc("junkA", [D], BF16),
                      "ss8": A.alloc("ss8A", [TTH], F32)}
                S.mark("L1h%d:A" % half)
                phaseA(1, half, XA, preloaded=(4 if half == 1 else 0))
                S.mark("L1h%d:B0" % half)
                if dbg == "L1:%d:A" % half:
                    S.stopped = True
                if os.environ.get("KLIMIT"):
                    print("L1 B0 start nrec", S.nrec)
                S.barrier()
                A.off = mark_x1
                f2 = lambda n: A.alloc(n, [512], F32)
                rkvcol = A.alloc("rkvcol", [TTH], F32)
                CRq = A.alloc("CRq", [TH], F32, parts=64)
                SRq = A.alloc("SRq", [TH], F32, parts=64)
                tA = A.alloc("tA", [512], F32, parts=64)
                tB = A.alloc("tB", [512], F32, parts=64)
                wq = [A.alloc("wq%d" % i, [4, 256], BF16) for i in range(2)]
                wkv = [A.alloc("wkv%d" % i, [4, 256], BF16) for i in range(2)]
                mark_b1 = A.off

                def load_w(h):
                    if h >= 16:
                        return
                    if h % 4 == 0:
                        hg = h // 4
                        dma_in("pool", wslab[hg % 2], w1v[:, :, 1088 + hg * 512:1088 + (hg + 1) * 512])
                    w_ = wq[h % 2]
                    dma_in("pool", w_[:, :, 0:192], wqv[:, :, h * 192:(h + 1) * 192])
                    dma_in("pool", w_[:, :, 192:224], wqv[:, :, h * 192 + 160:h * 192 + 192])
                    dma_in("pool", w_[:, :, 224:256], wqv[:, :, h * 192 + 128:h * 192 + 160])
                    dma_in("pool", wkv[h % 2], wkvv[:, :, h * 256:(h + 1) * 256])

                wkpe = A.alloc("wkpe", [16, 128], BF16)
                sqA = [A.alloc("sqA%d" % i, [512], BF16) for i in range(2)]
                sdl = f2("sdl")
                posi = A.alloc("posi", [TH], I32, parts=64)
                uu = A.alloc("uu", [TH], F32, parts=64)
                ki = A.alloc("ki", [TH], I32, parts=64)
                kf = A.alloc("kf", [TH], F32, parts=64)

                dma_in("sp", posi, pos_d[half * TH:(half + 1) * TH].partition_broadcast(64))
                cp("dve", uu, posi)
                ts("dve", uu, uu, invcol[:, 0:1], None, ALU.mult)
                for tab, shift in ((Stab, 0.0), (Ctab, 0.25)):
                    if shift:
                        ts("dve", kf, uu, shift, None, ALU.add)
                        src_u = kf
                    else:
                        src_u = uu
                    cp("dve", ki, src_u)
                    cp("dve", tab, ki)
                    tt("dve", tab, src_u, tab, ALU.subtract)
                    ts("dve", kf, tab, 0.5, None, ALU.is_gt)
                    tt("dve", tab, tab, kf, ALU.subtract)
                    act(tab, tab, AF.Sin, scale=6.283185)
                ts("dve", Stab[0:32, :], Stab[0:32, :], -1.0, None, ALU.mult)

                if os.environ.get("KLIMIT"):
                    print("L1 B0 tables done nrec", S.nrec)
                dma_in("pool", wkpe[:, :, 0:64], w1v[:, :, 1024:1088])
                dma_in("pool", wkpe[:, :, 64:96], w1v[:, :, 1056:1088])
                dma_in("pool", wkpe[:, :, 96:128], w1v[:, :, 1024:1056])
                for blk in range(NBH):
                    tok = slice(blk * 512, (blk + 1) * 512)
                    gtok = slice(half * TH + blk * 512, half * TH + (blk + 1) * 512)
                    for (wsl, dstT, rst, ncol0) in ((wslab[0], cqsT, rstdq, C_QN), (wslab[1], ckvsT, rstdkv, C_KVN)):
                        for g in range(4):
                            bank = nextbank()
                            inproj(bank, wsl, g * 128, blk)
                            act(sqA[g % 2], bank, AF.Square)
                            act(dstT[:, g, tok], bank, AF.Identity, scale=cols[:, ncol0 + g:ncol0 + g + 1])
                            mm(pb[7], onesb, sqA[g % 2], start=(g == 0), stop=(g == 3))
                        act(sdl, pb[7], AF.Ln, bias=EPS, scale=1.0 / 512)
                        act(rst[:, tok], sdl, AF.Exp, scale=-0.5)
                    for tl in range(4):
                        tr(pb[6][:, tl * 128:(tl + 1) * 128], rstdkv[:, blk * 512 + tl * 128:blk * 512 + (tl + 1) * 128], identf)
                    cp("dve", rkvcol[:, blk * 4:(blk + 1) * 4], pb[6].re("p (j d) -> p j d", j=4)[:, :, 0])
                    bka = nextbank()
                    inproj(bka[0:64, :], wkpe, 0, blk, m=64)
                    bkb = nextbank()
                    inproj(bkb[0:64, :], wkpe, 64, blk, m=64)
                    tt("dve", tA, bka[0:64, :], Ctab[:, tok], ALU.mult)
                    tt("dve", tB, bkb[0:64, :], Stab[:, tok], ALU.mult)
                    tt("dve", kpeT[:, gtok], tA, tB, ALU.add)
                    tt("dve", CRq[:, tok], Ctab[:, tok], rstdq[0:64, tok], ALU.mult)
                    tt("dve", SRq[:, tok], Stab[:, tok], rstdq[0:64, tok], ALU.mult)

                load_w(0)
                if dbg == "L1:%d:B0" % half:
                    S.stopped = True
                S.barrier()
                A.off = mark_b1
                qnT = [A.alloc("qnT%d" % i, [TH], BF16) for i in range(2)]
                qrT = [A.alloc("qrT%d" % i, [TH], BF16, parts=64) for i in range(2)]
                knT = [A.alloc("knT%d" % i, [TH], BF16) for i in range(2)]
                Vtok = [A.alloc("Vtok%d" % i, [TTH, 128], BF16) for i in range(2)]
                knP = [A.alloc("knP%d" % i, [TH], BF16) for i in range(2)]
                VP = [A.alloc("VP%d" % i, [TTH, 128], BF16) for i in range(2)]
                sgate = [A.alloc("sgate%d" % i, [NBH, 512], F32) for i in range(2)]
                Eb = [A.alloc("Eb%d" % i, [512], BF16) for i in range(4)]
                rden = f2("rden")
                pbanks = [pb[0], pb[1], pb[2]]
                sbanks = [pb[3], pb[4], pb[7]]
                prot = [0]

                def pbank():
                    b = pbanks[prot[0] % 3]
                    prot[0] += 1
                    return b

                def proj(h):
                    p = h % 2
                    hg, hh = h // 4, h % 4
                    gslab = wslab[hg % 2]
                    load_w(h + 1)
                    if half == 1:
                        dma_in("sp", knP[p], kvs_d[h, 0], reads=[kvs_bufs[h]])
                        dma_in("sp", VP[p].re("p a b -> p (a b)"), kvs_d[h, 1], reads=[kvs_bufs[h]])
                    for blk in range(NBH):
                        tok = slice(blk * 512, (blk + 1) * 512)
                        bank = pbank()
                        for rc in range(4):
                            mm(bank, wq[p][:, rc, 0:128], cqsT[:, rc, tok], start=(rc == 0), stop=(rc == 3))
                        tt("dve", qnT[p][:, tok], bank, rstdq[:, tok], ALU.mult)
                        bka = pbank()
                        for rc in range(4):
                            mm(bka[0:64, :], wq[p][:, rc, 128:192], cqsT[:, rc, tok], start=(rc == 0), stop=(rc == 3))
                        bkb = pbank()
                        for rc in range(4):
                            mm(bkb[0:64, :], wq[p][:, rc, 192:256], cqsT[:, rc, tok], start=(rc == 0), stop=(rc == 3))
                        tt("dve", tA, bka[0:64, :], CRq[:, tok], ALU.mult)
                        tt("dve", tB, bkb[0:64, :], SRq[:, tok], ALU.mult)
                        tt("pool", qrT[p][:, tok], tA, tB, ALU.add)
                        bank = pbank()
                        for rc in range(4):
                            mm(bank, wkv[p][:, rc, 0:128], ckvsT[:, rc, tok], start=(rc == 0), stop=(rc == 3))
                        tt("dve", knT[p][:, tok], bank, rstdkv[:, tok], ALU.mult)
                        bank = pbank()
                        bv = bank.re("p (j d) -> p j d", j=4)
                        for tl in range(4):
                            t0 = blk * 512 + tl * 128
                            for rc in range(4):
                                mm(bv[:, tl, :], ckvsT[:, rc, t0:t0 + 128], wkv[p][:, rc, 128:256],
                                   start=(rc == 0), stop=(rc == 3))
                        for tl in range(4):
                            ti = blk * 4 + tl
                            act(Vtok[p][:, ti, :], bv[:, tl, :], AF.Copy, scale=rkvcol[:, ti:ti + 1])
                        bank = pbank()
                        inproj(bank, gslab, hh * 128, blk)
                        act(sgate[p][:, blk, :], bank, AF.Silu)
                    if half == 0:
                        dma_out("sp", kvs_d[h, 0], knT[p], writes=[kvs_bufs[h]])
                        dma_out("sp", kvs_d[h, 1], Vtok[p].re("p a b -> p (a b)"), writes=[kvs_bufs[h]])

                def attn(h):
                    p = h % 2
                    for qb in range(NBH):
                        qtok0 = qb * 512
                        tiles = []
                        if half == 1:
                            for kt in range(TTH):
                                tiles.append(("p", kt, 0))
                        for kt in range(4 * qb + 4):
                            r = kt - 4 * qb
                            tiles.append(("l", kt, max(r, -1)))
                        po = pb[5]
                        pd = pb[6]
                        nt = len(tiles)
                        Es = [None] * nt

                        def emit_s(i):
                            kind, kt, r = tiles[i]
                            c0 = 128 * r if r > 0 else 0
                            ps = sbanks[i % 3]
                            if kind == "p":
                                kn_src, kg = knP[p], kt
                            else:
                                kn_src, kg = knT[p], half * TTH + kt
                            mm(ps[:, c0:512], kn_src[:, kt * 128:(kt + 1) * 128], qnT[p][:, qtok0 + c0:qtok0 + 512], start=True, stop=False)
                            mm(ps[:, c0:512], kpeT[:, kg * 128:(kg + 1) * 128], qrT[p][:, qtok0 + c0:qtok0 + 512], start=False, stop=True)
                            E = Eb[i % 4]
                            act(E[:, c0:512], ps[:, c0:512], AF.Exp, scale=SCALE)
                            if kind == "l" and r >= 0:
                                tt("pool", E[:, c0:c0 + 128], E[:, c0:c0 + 128], tri128b, ALU.mult)
                            Es[i] = (E, c0)

                        def emit_pv(i):
                            kind, kt, r = tiles[i]
                            E, c0 = Es[i]
                            v_src = VP[p] if kind == "p" else Vtok[p]
                            mm(po[:, c0:512], v_src[:, kt, :], E[:, c0:512], start=(i == 0), stop=(i == nt - 1))
                            mm(pd[:, c0:512], onesb, E[:, c0:512], start=(i == 0), stop=(i == nt - 1))

                        emit_s(0)
                        if nt > 1:
                            emit_s(1)
                        for i in range(nt):
                            if i + 2 < nt:
                                emit_s(i + 2)
                            emit_pv(i)
                        act(rden, pd, AF.Ln)
                        act(rden, rden, AF.Exp, scale=-1.0)
                        tt("dve", rden, po, rden, ALU.mult)
                        tt("dve", V(outT_b[h][qb], outT.ap[:, h, qtok0:qtok0 + 512]), rden, sgate[p][:, qb, :], ALU.mult)

                S.mark("L1h%d:B1" % half)
                proj(0)
                for h in range(16):
                    if h + 1 < 16:
                        proj(h + 1)
                    attn(h)

                if dbg == "L1B" or dbg == "L1:%d:B1" % half:
                    S.stopped = True
                phaseC_preload(od_w_out)
                S.barrier()
                A.off = mark_xa1
                XC = {"yacc": None,
                      "xt": xt4,
                      "ss8": A.alloc("ss8C", [TTH], F32),
                      "ssq": A.alloc("ssqC", [TTH, 4], F32),
                      "junk": A.alloc("junkC", [512], BF16)}
                ytiles = []
                for i in range(TTH):
                    if i < TTH // 2:
                        ytiles.append(Buf(A.t[:, hT.off + i * D:hT.off + (i + 1) * D], "yaccL%d" % i))
                    else:
                        j = i - TTH // 2
                        ytiles.append(Buf(A.t[:, lat_off + j * D:lat_off + (j + 1) * D], "yaccU%d" % i))
                XC["yacc"] = _Y(ytiles)
                S.mark("L1h%d:C" % half)
                phaseC(1, half, XC, od_w_out, out_d, is_final=True,
                       after=(preload_next_A1 if half == 0 else None))
                if dbg == "L1:%d:C" % half:
                    S.stopped = True

        S.mark("END")
        S.emit(nc, st)
    nc._marks = S.marks
    return nc


_PROG = {}


def _get_prog(layers):
    if layers not in _PROG:
        _PROG[layers] = build_program(layers)
    return _PROG[layers]


def _in_maps_l0(inputs, xs):
    maps = []
    for c in range(8):
        b = c % 4
        maps.append({
            "x": np.ascontiguousarray(xs[b]),
            "norm_pre": np.ascontiguousarray(inputs["norm_pre"]),
            "norm_post": np.ascontiguousarray(inputs["norm_post"]),
            "ev_w_in": np.ascontiguousarray(inputs["ev_w_in"][0]),
            "ev_lb_logits": np.ascontiguousarray(inputs["ev_lb_logits"].reshape(16, 128)),
            "ev_a_onorm": np.ascontiguousarray(inputs["ev_a_onorm"].reshape(1, 128)),
            "ev_b_ln_w": np.ascontiguousarray(inputs["ev_b_ln_w"].reshape(8, 128)),
            "ev_b_ln_b": np.ascontiguousarray(inputs["ev_b_ln_b"].reshape(8, 128)),
            "ev_b_ws": np.ascontiguousarray(inputs["ev_b_ws"][0]),
            "ev_b_bias": np.ascontiguousarray(inputs["ev_b_bias"].reshape(1024)),
            "ev_w_out": np.ascontiguousarray(inputs["ev_w_out"][0]),
        })
    return maps


def _in_maps_l1(inputs, xs):
    maps = []
    for c in range(8):
        b = c % 4
        maps.append({
            "x": np.ascontiguousarray(xs[b]),
            "positions": np.ascontiguousarray(inputs["positions"][b]).astype(np.int32),
            "norm_pre": np.ascontiguousarray(inputs["norm_pre"]),
            "norm_post": np.ascontiguousarray(inputs["norm_post"]),
            "od_w_in": np.ascontiguousarray(inputs["od_w_in"][0]),
            "od_q_norm": np.ascontiguousarray(inputs["od_q_norm"].reshape(4, 128)),
            "od_w_qb": np.ascontiguousarray(inputs["od_w_qb"][0]),
            "od_kv_norm": np.ascontiguousarray(inputs["od_kv_norm"].reshape(4, 128)),
            "od_w_kvb": np.ascontiguousarray(inputs["od_w_kvb"][0]),
            "od_w_out": np.ascontiguousarray(inputs["od_w_out"][0]),
        })
    return maps


FUSED = True


def kernel(**inputs):
    inputs = {k: np.asarray(v) for k, v in inputs.items()}
    if FUSED:
        nc = _get_prog((0, 1))
        m0 = _in_maps_l0(inputs, inputs["x"])
        m1 = _in_maps_l1(inputs, inputs["x"])
        maps = [dict(a, **b) for a, b in zip(m1, m0)]
        res = run_bass_kernel_spmd(nc, maps, core_ids=list(range(8)))
        return np.stack([res.results[b]["out"] for b in range(4)]).astype(np.float32)
    nc0 = _get_prog((0,))
    res0 = run_bass_kernel_spmd(nc0, _in_maps_l0(inputs, inputs["x"]), core_ids=list(range(8)))
    x1 = [res0.results[b]["out"] for b in range(4)]
    nc1 = _get_prog((1,))
    res1 = run_bass_kernel_spmd(nc1, _in_maps_l1(inputs, x1), core_ids=list(range(8)))
    return np.stack([res1.results[b]["out"] for b in range(4)]).astype(np.float32)


if __name__ == "__main__":
    import time
    t0 = time.time()
    nc = build_program((0,))
    print("built", time.time() - t0)
```

```python
import contextlib
import math
import numpy as np
import concourse.bass as bass
import concourse.mybir as mybir
from concourse.bass_utils import run_bass_kernel_spmd

F32 = mybir.dt.float32
BF16 = mybir.dt.bfloat16
I32 = mybir.dt.int32
AF = mybir.ActivationFunctionType
ALU = mybir.AluOpType
AX = mybir.AxisListType

ENGS = ("sp", "pe", "act", "dve", "pool")
D = 2048
TFULL = 2048
TH = 1024
NBH = TH // 512
TTH = TH // 128
EPS = 1e-6


class V:
    def __init__(self, buf, ap):
        self.buf = buf
        self.ap = ap

    def __getitem__(self, idx):
        return V(self.buf, self.ap[idx])

    def re(self, pat, **kw):
        return V(self.buf, self.ap.rearrange(pat, **kw))

    def bc(self, shape):
        return V(self.buf, self.ap.to_broadcast(shape))

    def cast(self, dt):
        return V(self.buf, self.ap.bitcast(dt))


class Buf(V):
    def __init__(self, ap, name=""):
        self.buf = self
        self.ap = ap
        self.name = name
        self.wl = []
        self.r = {}


class Sched:
    def __init__(self, ring=12):
        self.ops = {e: [] for e in ENGS}
        self.count = {e: 0 for e in ENGS}
        self.seen = {e: {} for e in ENGS}
        self.rings = {"sp": [("dma", "sp", i) for i in range(ring)],
                      "pool": [("dma", "pool", i) for i in range(6)]}
        self.ring_pos = {q: 0 for q in self.rings}
        self.dma_tot = {}
        for q in self.rings:
            for k in self.rings[q]:
                self.dma_tot[k] = 0
        self.out_dmas = []
        self.epoch = {}
        self.epoch_seen = {e: True for e in ENGS}

    def mark(self, name):
        if not hasattr(self, "marks"):
            self.marks = []
        self.marks.append((name, dict(self.count)))

    def barrier(self):
        ep = {}
        for e in ENGS:
            if self.count[e] > 0:
                ep[("eng", e)] = self.count[e]
        for k, v in self.dma_tot.items():
            if v > 0:
                ep[k] = v
        self.epoch = ep
        self.epoch_seen = {e: False for e in ENGS}

    def _deps(self, eng, reads, writes, nowaw_q=None):
        deps = []
        if not self.epoch_seen[eng]:
            self.epoch_seen[eng] = True
            deps.extend(self.epoch.items())
        for b in reads:
            deps.extend(b.wl)
        for b in writes:
            if not (nowaw_q is not None):
                deps.extend(b.wl)
            else:
                deps.extend(x for x in b.wl if not (x[0][0] == "dma" and x[0][1] == nowaw_q))
            deps.extend(b.r.values())
        return deps

    def _need(self, eng, deps):
        best = {}
        for k, v in deps:
            if k == ("eng", "pe") and eng == "pe":
                continue
            if best.get(k, 0) < v:
                best[k] = v
        waits = []
        for k, v in best.items():
            if self.seen[eng].get(k, 0) < v:
                self.seen[eng][k] = v
                waits.append((k, v))
        return waits

    stopped = False
    nrec = 0
    limit = 0

    def op(self, eng, fn, reads=(), writes=()):
        self.nrec += 1
        if self.stopped or (self.limit and self.nrec > self.limit):
            return None
        waits = self._need(eng, self._deps(eng, reads, writes))
        self.count[eng] += 1
        me = (("eng", eng), self.count[eng])
        self.ops[eng].append((waits, fn, me, 1))
        for b in reads:
            b.r[me[0]] = me
        for b in writes:
            b.wl = [me]
            b.r = {}
        return me

    def dma(self, q, fn, reads=(), writes=(), is_output=False, force=False, nowaw=False):
        self.nrec += 1
        if not force and (self.stopped or (self.limit and self.nrec > self.limit)):
            return None
        deps = self._deps(q, reads, writes, nowaw_q=(q if nowaw else None))
        ring = self.rings[q]
        k = ring[self.ring_pos[q] % len(ring)]
        self.ring_pos[q] += 1
        if self.dma_tot[k] > 0:
            deps.append((k, self.dma_tot[k]))
        waits = self._need(q, deps)
        self.dma_tot[k] += 16
        me = (k, self.dma_tot[k])
        self.ops[q].append((waits, fn, me, 16))
        for b in reads:
            b.r[k] = me
        for b in writes:
            if nowaw:
                b.wl = [x for x in b.wl if (x[0][0] == "dma" and x[0][1] == q)] + [me]
            else:
                b.wl = [me]
            b.r = {}
        if is_output:
            self.out_dmas.append(me)
        return me

    def emit(self, nc, stack):
        sems = {}
        for e in ENGS:
            sems[("eng", e)] = stack.enter_context(nc.semaphore("s_" + e))
        for q in self.rings:
            for k in self.rings[q][: min(len(self.rings[q]), self.ring_pos[q])]:
                sems[k] = stack.enter_context(nc.semaphore("d_%s_%d" % (k[1], k[2])))
        final_waits = {}
        for k, v in self.out_dmas:
            final_waits[k] = max(final_waits.get(k, 0), v)
        for k, v in self.dma_tot.items():
            if v > 0:
                final_waits[k] = max(final_waits.get(k, 0), v)
        for e in ENGS:
            if e != "sp" and self.count[e] > 0:
                final_waits[("eng", e)] = self.count[e]

        def replay(name, eng):
            for waits, fn, me, amt in self.ops[name]:
                for k, v in waits:
                    eng.wait_ge(sems[k], v)
                ins = fn(eng)
                ins.then_inc(sems[me[0]], amt)
            if name == "sp":
                for k, v in final_waits.items():
                    eng.wait_ge(sems[k], v)

        block = stack.enter_context(nc.Block())

        @block.sync
        def _(e):
            replay("sp", e)

        @block.tensor
        def _(e):
            replay("pe", e)

        @block.scalar
        def _(e):
            replay("act", e)

        @block.vector
        def _(e):
            replay("dve", e)

        @block.gpsimd
        def _(e):
            replay("pool", e)


class Arena:
    def __init__(self, nc, stack, kbytes):
        self.n = kbytes * 256
        self.t = stack.enter_context(nc.sbuf_tensor("arena", [128, self.n], F32))
        self.off = 0

    def alloc(self, name, shape, dt=F32, parts=128):
        nel = 1
        for s in shape:
            nel *= s
        nbytes = nel * (2 if dt == BF16 else 4)
        ncol = (nbytes + 3) // 4
        assert self.off + ncol <= self.n, "arena overflow at %s: %d + %d > %d" % (name, self.off, ncol, self.n)
        ap = self.t[0:parts, self.off:self.off + ncol]
        off0 = self.off
        self.off += ncol
        if dt != F32:
            ap = ap.bitcast(dt)
        if len(shape) == 2:
            ap = ap.rearrange("p (a b) -> p a b", a=shape[0])
        elif len(shape) == 3:
            ap = ap.rearrange("p (a b c) -> p a b c", a=shape[0], b=shape[1])
        b = Buf(ap, name)
        b.off = off0
        b.ncol = ncol
        return b


class _Stop(Exception):
    pass


class _Y:
    def __init__(self, ytiles):
        self.ytiles = ytiles

    def __getitem__(self, idx):
        p, i, n = idx
        return self.ytiles[i][:, n]


def build_program(layers=(0, 1), dbg=None):
    do0 = 0 in layers
    do1 = 1 in layers
    nc = bass.Bass("TRN2", target_bir_lowering=False)
    S = Sched()
    import os
    S.limit = int(os.environ.get("KLIMIT", "0"))

    def din(name, shape, dt=F32):
        return nc.dram_tensor(name, shape, dt, kind="ExternalInput").ap()

    x_d = din("x", [TFULL, D])
    out_d = nc.dram_tensor("out", [TFULL, D], F32, kind="ExternalOutput").ap()
    if do0 and do1:
        x1_d = nc.dram_tensor("x1s", [TFULL, D], F32).ap()
    elif do0:
        x1_d = out_d
    else:
        x1_d = x_d
    npre_d = din("norm_pre", [2, D])
    npost_d = din("norm_post", [2, D])
    if do0:
        ev_w_in = din("ev_w_in", [D, 7168])
        ev_lb = din("ev_lb_logits", [16, 128])
        ev_onorm = din("ev_a_onorm", [1, 128])
        ev_lnw = din("ev_b_ln_w", [8, 128])
        ev_lnb = din("ev_b_ln_b", [8, 128])
        ev_ws = din("ev_b_ws", [8, 128, 128])
        ev_bias = din("ev_b_bias", [1024])
        ev_w_out = din("ev_w_out", [D, D])
    if do1:
        pos_d = din("positions", [TFULL], I32)
        od_w_in = din("od_w_in", [D, 3136])
        od_qn = din("od_q_norm", [4, 128])
        od_wqb = din("od_w_qb", [512, 3072])
        od_kvn = din("od_kv_norm", [4, 128])
        od_wkvb = din("od_w_kvb", [512, 4096])
        od_w_out = din("od_w_out", [D, D])
        kvs_d = nc.dram_tensor("kvs", [16, 2, 128, TH], BF16).ap()

    dbg_d = nc.dram_tensor("dbg", [128, 16 * TH], BF16, kind="ExternalOutput").ap() if (dbg and not dbg.startswith("L1:")) else None
    x1_tiles = [Buf(None, "x1t%d" % i) for i in range(TFULL // 128)]
    kvs_bufs = [Buf(None, "kvs%d" % i) for i in range(16)]

    with contextlib.ExitStack() as st:
        A = Arena(nc, st, 207)
        pb = [Buf(st.enter_context(nc.psum_tensor("pb%d" % i, [128, 512], F32))[:, :], "pb%d" % i) for i in range(8)]

        def mm(out, lhsT, rhs, start=True, stop=True, reads=()):
            S.op("pe", lambda e: e.matmul(out.ap, lhsT=lhsT.ap, rhs=rhs.ap, start=start, stop=stop),
                 reads=[lhsT.buf, rhs.buf, *reads], writes=[out.buf])

        def tr(out, in_, ident):
            S.op("pe", lambda e: e.transpose(out=out.ap, in_=in_.ap, identity=ident.ap),
                 reads=[in_.buf, ident.buf], writes=[out.buf])

        def act(out, in_, func, bias=None, scale=None, accum=None, reads=(), junk_out=False):
            kw = {}
            rd = [in_.buf, *reads]
            wr = [] if junk_out else [out.buf]
            if bias is not None:
                if isinstance(bias, V):
                    kw["bias"] = bias.ap
                    rd.append(bias.buf)
                else:
                    kw["bias"] = float(bias)
            if scale is not None:
                if isinstance(scale, V):
                    kw["scale"] = scale.ap
                    rd.append(scale.buf)
                else:
                    kw["scale"] = float(scale)
            if accum is not None:
                kw["accum_out"] = accum.ap
                wr.append(accum.buf)
            S.op("act", lambda e: e.activation(out=out.ap, in_=in_.ap, func=func, **kw), reads=rd, writes=wr)

        def tt(eng, out, in0, in1, op):
            S.op(eng, lambda e: e.tensor_tensor(out=out.ap, in0=in0.ap, in1=in1.ap, op=op),
                 reads=[in0.buf, in1.buf], writes=[out.buf])

        def ts(eng, out, in0, s1, s2, op0, op1=None):
            rd = [in0.buf]
            a1 = s1
            a2 = s2
            if isinstance(s1, V):
                rd.append(s1.buf)
                a1 = s1.ap
            if isinstance(s2, V):
                rd.append(s2.buf)
                a2 = s2.ap
            if op1 is None:
                S.op(eng, lambda e: e.tensor_scalar(out=out.ap, in0=in0.ap, scalar1=a1, scalar2=None, op0=op0),
                     reads=rd, writes=[out.buf])
            else:
                S.op(eng, lambda e: e.tensor_scalar(out=out.ap, in0=in0.ap, scalar1=a1, scalar2=a2, op0=op0, op1=op1),
                     reads=rd, writes=[out.buf])

        def stt(out, in0, scalar, in1, op0, op1):
            rd = [in0.buf, in1.buf]
            a = scalar
            if isinstance(scalar, V):
                rd.append(scalar.buf)
                a = scalar.ap
            S.op("dve", lambda e: e.scalar_tensor_tensor(out=out.ap, in0=in0.ap, scalar=a, in1=in1.ap, op0=op0, op1=op1),
                 reads=rd, writes=[out.buf])

        def cp(eng, out, in_):
            if eng == "act":
                S.op("act", lambda e: e.activation(out=out.ap, in_=in_.ap, func=AF.Copy), reads=[in_.buf], writes=[out.buf])
            else:
                S.op(eng, lambda e: e.tensor_copy(out=out.ap, in_=in_.ap), reads=[in_.buf], writes=[out.buf])

        def recip(out, in_):
            S.op("dve", lambda e: e.reciprocal(out=out.ap, in_=in_.ap), reads=[in_.buf], writes=[out.buf])

        def red(out, in_):
            S.op("dve", lambda e: e.tensor_reduce(out=out.ap, in_=in_.ap, axis=AX.X, op=ALU.add),
                 reads=[in_.buf], writes=[out.buf])

        def memset(eng, out, val):
            S.op(eng, lambda e: e.memset(out.ap, val), writes=[out.buf])

        def dma_in(q, out, src_ap, reads=(), nowaw=False):
            S.dma(q, lambda e: e.dma_start(out=out.ap, in_=src_ap), reads=list(reads), writes=[out.buf], nowaw=nowaw)

        def dma_out(q, dst_ap, in_, writes=(), is_output=False):
            S.dma(q, lambda e: e.dma_start(out=dst_ap, in_=in_.ap), reads=[in_.buf], writes=list(writes), is_output=is_output)

        identf = A.alloc("identf", [128], F32)
        identb = A.alloc("identb", [128], BF16)
        tri64 = A.alloc("tri64", [64], F32)
        tri128b = A.alloc("tri128b", [128], BF16)
        wsmask = A.alloc("wsmask", [128], F32)
        ones1 = A.alloc("ones1", [128], F32)
        onesb = A.alloc("onesb", [128], BF16)
        cmask = A.alloc("cmask", [512], BF16)
        pst = A.alloc("pst", [128], F32)
        cols = A.alloc("cols", [48], F32)
        tmpc = A.alloc("tmpc", [128], F32)

        memset("pool", identf, 1.0)
        S.op("pool", lambda e: e.affine_select(out=identf.ap, in_=identf.ap, pattern=[[-1, 128]], compare_op=ALU.is_equal,
                                               fill=0.0, base=0, channel_multiplier=1), reads=[identf], writes=[identf])
        cp("dve", identb, identf)
        memset("pool", tri64, 1.0)
        S.op("pool", lambda e: e.affine_select(out=tri64[0:64, :].ap, in_=tri64[0:64, :].ap, pattern=[[1, 64]],
                                               compare_op=ALU.is_ge, fill=0.0, base=0, channel_multiplier=-1),
             reads=[tri64], writes=[tri64])
        memset("pool", tmpc, 1.0)
        S.op("pool", lambda e: e.affine_select(out=tmpc.ap, in_=tmpc.ap, pattern=[[1, 128]], compare_op=ALU.is_ge,
                                               fill=0.0, base=0, channel_multiplier=-1), reads=[tmpc], writes=[tmpc])
        cp("dve", tri128b, tmpc)
        memset("pool", wsmask, 1.0)
        S.op("pool", lambda e: e.affine_select(out=wsmask.ap, in_=wsmask.ap, pattern=[[-1, 128]], compare_op=ALU.is_ge,
                                               fill=0.0, base=0, channel_multiplier=1), reads=[wsmask], writes=[wsmask])
        memset("pool", ones1, 1.0)
        memset("pool", onesb, 1.0)
        memset("pool", cmask, 1.0)
        memset("pool", cmask.re("p (j t) -> p j t", j=8)[:, :, 0:1], 0.0)
        memset("pool", pst, 0.0)

        if do0:
            dma_in("sp", pst[0:16, :], ev_lb)
            dma_in("sp", pst[16:17, :], ev_onorm)
            dma_in("sp", pst[17:25, :], ev_lnw)
            dma_in("sp", pst[25:33, :], ev_lnb)
        if do1:
            dma_in("sp", pst[33:37, :], od_qn)
            dma_in("sp", pst[37:41, :], od_kvn)
        tr(pb[0][:, 0:48], pst[0:48, :], identf[0:48, 0:48])
        cp("dve", cols, pb[0][:, 0:48])
        C_LB, C_ON, C_LNW, C_LNB, C_QN, C_KVN = 0, 16, 17, 25, 33, 37

        wbc = A.alloc("wbc", [D], F32)
        hT = A.alloc("hT", [16, TH], BF16)
        hT_tiles = [Buf(hT.ap, "hT%d" % i) for i in range(TTH)]
        outT = A.alloc("outT", [16, TH], BF16)
        outT_b = [[Buf(outT.ap, "outT%d_%d" % (c, b)) for b in range(NBH)] for c in range(16)]
        wslab = [A.alloc("wslab%d" % i, [16, 512], BF16) for i in range(2)]
        mark_layer = A.off
        if dbg and not dbg.startswith("L1:"):
            memset("pool", outT, 0.0)

        bank_rot = [0]

        def nextbank():
            b = pb[bank_rot[0] % 3]
            bank_rot[0] += 1
            return b

        def inproj(dst, w, col0, blk, m=128):
            tok = slice(blk * 512, (blk + 1) * 512)
            tiles = hT_tiles[4 * blk:4 * blk + 4]
            for c in range(16):
                mm(dst, w[:, c, col0:col0 + m], V(tiles[0], hT.ap[:, c, tok]), start=(c == 0), stop=(c == 15),
                   reads=tiles[1:])

        def phaseA(layer, half, XA, preloaded=0):
            src = x_d if layer == 0 else x1_d
            dma_in("sp", wbc, npre_d[layer, :].partition_broadcast(128))
            ss8 = XA["ss8"]
            junk = XA["junk"]
            k = 0
            def tile_out(i, j):
                xt = XA["xt"][j]
                hb = XA["hb"][i % 2]
                stt(hb, xt, ss8[:, i:i + 1], wbc, ALU.mult, ALU.mult)
                tpa = pb[(2 * kk_[0]) % 8].cast(BF16).re("p (c d) -> p c d", c=8)
                tpb = pb[(2 * kk_[0] + 1) % 8].cast(BF16).re("p (c d) -> p c d", c=8)
                kk_[0] += 1
                for c in range(16):
                    dst = tpa if c < 8 else tpb
                    tr(dst[:, c % 8, :], hb[:, c * 128:(c + 1) * 128], identb)
                tl = hT_tiles[i]
                cp("act", V(tl, hT.ap[:, 0:8, i * 128:(i + 1) * 128]), tpa)
                cp("dve", V(tl, hT.ap[:, 8:16, i * 128:(i + 1) * 128]), tpb)

            def rstd_of(sl):
                ts("dve", sl, sl, 1.0 / D, EPS, ALU.mult, ALU.add)
                act(sl, sl, AF.Sqrt)
                recip(sl, sl)

            kk_ = [0]
            for wv_ in range(TTH // 4):
                for j in range(4):
                    i = wv_ * 4 + j
                    gi = half * TTH + i
                    xt = XA["xt"][j]
                    rd = [x1_tiles[gi]] if layer == 1 else []
                    if i >= preloaded:
                        dma_in("sp", xt, src[gi * 128:(gi + 1) * 128, :], reads=rd)
                    act(junk, xt, AF.Square, accum=ss8[:, i:i + 1], junk_out=True)
                    if i == 0:
                        rstd_of(ss8[:, 0:1])
                if wv_ == 0:
                    tile_out(0, 0)
                    rstd_of(ss8[:, 1:4])
                    first = 1
                else:
                    rstd_of(ss8[:, wv_ * 4:(wv_ + 1) * 4])
                    first = 0
                for j in range(first, 4):
                    tile_out(wv_ * 4 + j, j)

        def phaseC_preload(w_out_d):
            wv = w_out_d.rearrange("(c p) n -> p c n", p=128)
            for nb in range(2):
                dma_in("pool", wslab[nb % 2], wv[:, :, nb * 512:(nb + 1) * 512])

        def phaseC(layer, half, XC, w_out_d, dst_d, is_final, after=None):
            src = x_d if layer == 0 else x1_d
            dma_in("sp", wbc, npost_d[layer, :].partition_broadcast(128))
            yacc = XC["yacc"]
            ss8 = XC["ss8"]
            ssq = XC["ssq"]
            junk = XC["junk"]
            wv = w_out_d.rearrange("(c p) n -> p c n", p=128)
            nxt = len(XC["xt"])

            def load_x(i):
                gi = half * TTH + i
                rd = [x1_tiles[gi]] if layer == 1 else []
                dma_in("sp", XC["xt"][i % nxt], src[gi * 128:(gi + 1) * 128, :], reads=rd)

            for i in range(nxt):
                load_x(i)
            k = 0
            for nb in range(4):
                w = wslab[nb % 2]
                if nb >= 2:
                    dma_in("pool", w, wv[:, :, nb * 512:(nb + 1) * 512])
                for i in range(TTH):
                    bank = pb[k % 8]
                    k += 1
                    for c in range(16):
                        mm(bank, V(outT_b[c][i // 4], outT.ap[:, c, i * 128:(i + 1) * 128]), w[:, c, :],
                           start=(c == 0), stop=(c == 15))
                    act(junk, bank, AF.Square, accum=ssq[:, i, nb:nb + 1], junk_out=True)
                    yv = yacc[:, i, nb * 512:(nb + 1) * 512]
                    wv_ = wbc[:, nb * 512:(nb + 1) * 512]
                    S.op("dve", (lambda o, a, b: (lambda e: e.tensor_tensor(out=o, in0=a, in1=b, op=ALU.mult)))(yv.ap, bank.ap, wv_.ap),
                         reads=[bank, wbc, ssq], writes=[yv.buf])
            tt("dve", ss8, ssq[:, :, 0], ssq[:, :, 1], ALU.add)
            tt("dve", ss8, ss8, ssq[:, :, 2], ALU.add)
            tt("dve", ss8, ss8, ssq[:, :, 3], ALU.add)
            ts("dve", ss8, ss8, 1.0 / D, EPS, ALU.mult, ALU.add)
            act(ss8, ss8, AF.Sqrt)
            recip(ss8, ss8)
            for i in range(TTH):
                gi = half * TTH + i
                xt = XC["xt"][i % nxt]
                yt = yacc[:, i, :]
                stt(yt, yt, ss8[:, i:i + 1], xt, ALU.mult, ALU.add)
                wr = [x1_tiles[gi]] if (layer == 0) else []
                dma_out("sp", dst_d[gi * 128:(gi + 1) * 128, :], yt, writes=wr, is_output=is_final)
                if i + nxt < TTH:
                    load_x(i + nxt)
                if i == nxt - 1 and after is not None:
                    after()

        def dump(buf16):
            S.dma("sp", lambda e: e.dma_start(out=dbg_d, in_=buf16.ap.rearrange("p c t -> p (c t)")), reads=[buf16] + hT_tiles + [b for r in outT_b for b in r], is_output=True, force=True)
            S.stopped = True

        if do0:
            lbc = A.alloc("lbc", [8], F32)
            omlc = A.alloc("omlc", [8], F32)
            nomlc = A.alloc("nomlc", [8], F32)
            wsT = A.alloc("wsT", [8, 128], BF16)
            Cg = A.alloc("Cg", [8, 128], F32)
            biasbc = A.alloc("biasbc", [8, 128], F32)
            Sfin = A.alloc("Sfin", [8, 128], F32)
            wsraw = [A.alloc("wsraw%d" % i, [128], F32) for i in range(2)]
            wsm = [A.alloc("wsm%d" % i, [128], BF16) for i in range(2)]
            mark_x = A.off
            xt4 = [A.alloc("xt4_%d" % i, [D], F32) for i in range(6)]
            mark_xa = A.off
            for j in range(4):
                dma_in("sp", xt4[j], x_d[j * 128:(j + 1) * 128, :])

            tt("dve", lbc, cols[:, C_LB:C_LB + 8], cols[:, C_LB + 8:C_LB + 16], ALU.subtract)
            act(lbc, lbc, AF.Sigmoid)
            ts("dve", omlc, lbc, -1.0, 1.0, ALU.mult, ALU.add)
            ts("dve", nomlc, omlc, -1.0, None, ALU.mult)
            dma_in("sp", biasbc, ev_bias.partition_broadcast(128).rearrange("p (g t) -> p g t", g=8))
            for g in range(8):
                dma_in("sp", wsraw[g % 2], ev_ws[g])
                tt("dve", wsm[g % 2], wsraw[g % 2], wsmask, ALU.mult)
                tpw = pb[1].cast(BF16)[:, 0:128]
                tr(tpw, wsm[g % 2], identb)
                cp("act", wsT[:, g, :], tpw)
                mm(pb[2][:, 0:128], onesb, wsT[:, g, :])
                stt(Cg[:, g, :], pb[2][:, 0:128], cols[:, C_LNB + g:C_LNB + g + 1], biasbc[:, g, :], ALU.mult, ALU.add)

            w0v = ev_w_in.rearrange("(c p) n -> p c n", p=128)

            def preload_next_A0():
                for j in range(4):
                    dma_in("sp", xt4[2 + j], x_d[(TTH + j) * 128:(TTH + j + 1) * 128, :])

            for half in range(2):
                slab_of = {}
                slab_cnt = [0]

                def load_slab(kind, idx):
                    key = (kind, idx)
                    if key in slab_of:
                        return
                    w = wslab[slab_cnt[0] % 2]
                    slab_cnt[0] += 1
                    slab_of[key] = w
                    if kind == "h":
                        for gi_, base in enumerate((1024, 0, 2048, 3072)):
                            c0 = base + idx * 128
                            dma_in("pool", w[:, :, gi_ * 128:(gi_ + 1) * 128], w0v[:, :, c0:c0 + 128], nowaw=(gi_ > 0))
                    else:
                        for gi_, base in enumerate((4096, 5120, 6144, 6144) if os.environ.get("G4") else (4096, 5120, 6144)):
                            c0 = base + idx * 128
                            dma_in("pool", w[:, :, gi_ * 128:(gi_ + 1) * 128], w0v[:, :, c0:c0 + 128], nowaw=(gi_ > 0))

                load_slab("h", 0)
                load_slab("h", 1)
                if half > 0:
                    S.barrier()
                A.off = mark_xa
                XA = {"xt": (xt4[0:4] if half == 0 else xt4[2:6]),
                      "hb": [A.alloc("hbA%d" % i, [D], BF16) for i in range(2)],
                      "junk": A.alloc("junkA", [D], BF16),
                      "ss8": A.alloc("ss8A", [TTH], F32)}
                S.mark("L0h%d:A" % half)
                phaseA(0, half, XA, preloaded=4)
                S.mark("L0h%d:B" % half)
                if dbg == "A":
                    dump(hT)
                S.barrier()
                A.off = mark_x
                f2 = lambda n: A.alloc(n, [512], F32)
                b2 = lambda n: A.alloc(n, [512], BF16)
                sig, logf, bb, kk, ek, dd = f2("sig"), f2("logf"), f2("bb"), f2("kk"), f2("ek"), f2("dd")
                ktil = [b2("ktil%d" % i) for i in range(2)]
                kdec = [b2("kdec%d" % i) for i in range(2)]
                vT = [b2("vT%d" % i) for i in range(2)]
                kdT = A.alloc("kdT", [8, 128], BF16, parts=64)
                eq = [f2("eq%d" % i) for i in range(2)]
                qtil = [b2("qtil%d" % i) for i in range(3)]
                sg = [f2("sg%d" % i) for i in range(3)]
                vtok = [A.alloc("vtok%d" % i, [8, 128], BF16, parts=64) for i in range(2)]
                scT = [A.alloc("scT%d" % i, [8, 64], BF16, parts=64) for i in range(2)]
                SbA = [A.alloc("SbA%d" % i, [8, 128], BF16) for i in range(3)]
                Sf = [A.alloc("Sf%d" % i, [128], F32) for i in range(2)]
                osqb, sd, t1 = b2("osqb"), f2("sd"), f2("t1")
                sqv = f2("sqv")
                vTf = f2("vTf")
                st4 = [A.alloc("st4_%d" % i, [4], F32) for i in range(6)]
                vn = A.alloc("vn", [4, 128], BF16)
                svs, sgb, usg = f2("svs"), f2("sgb"), f2("usg")

                units = []
                for hd in range(8):
                    for blk in range(NBH):
                        units.append(("h", hd, blk))
                for g in range(8):
                    for blk in range(NBH):
                        units.append(("g", g, blk))

                def next_slab_key(ui):
                    kind, idx, blk = units[ui]
                    for uj in range(ui + 1, len(units)):
                        if (units[uj][0], units[uj][1]) != (kind, idx):
                            return (units[uj][0], units[uj][1])
                    return None

                def front_a(ui):
                    _, hd, blk = units[ui]
                    par = ui % 2
                    p3 = ui % 3
                    gblk = half * NBH + blk
                    load_slab("h", hd)
                    if blk == 0:
                        nk = next_slab_key(ui)
                        if nk is not None:
                            load_slab(*nk)
                    w = slab_of[("h", hd)]
                    lb_c = lbc[:, hd:hd + 1]
                    oml_c = omlc[:, hd:hd + 1]
                    noml_c = nomlc[:, hd:hd + 1]
                    pf = nextbank()
                    inproj(pf, w, 0, blk)
                    act(sig, pf, AF.Sigmoid)
                    pg = nextbank()
                    inproj(pg, w, 384, blk)
                    act(sg[p3], pg, AF.Silu)
                    pq = nextbank()
                    inproj(pq, w, 128, blk)
                    act(logf, sig, AF.Ln, bias=lb_c, scale=oml_c)
                    ts("pool", kk, sig, noml_c, oml_c, ALU.mult, ALU.add)
                    S.op("dve", lambda e: e.tensor_tensor_scan(out=bb.ap, data0=cmask.ap, data1=logf.ap, initial=0.0,
                                                               op0=ALU.mult, op1=ALU.add),
                         reads=[cmask, logf], writes=[bb])
                    act(eq[par], bb, AF.Exp)
                    act(ek, bb, AF.Exp, scale=-1.0)
                    bb3 = bb.re("p (j t) -> p j t", j=8)
                    tt("dve", dd.re("p (j t) -> p j t", j=8), bb3, bb3[:, :, 63:64].bc([128, 8, 64]), ALU.subtract)
                    act(dd, dd, AF.Exp, scale=-1.0)
                    tt("dve", qtil[p3], pq, eq[par], ALU.mult)
                    pi_ = nextbank()
                    inproj(pi_, w, 256, blk)
                    tt("pool", ktil[par], kk, ek, ALU.mult)
                    tt("pool", kdec[par], kk, dd, ALU.mult)
                    cp("act", vT[par], pi_)

                def front_b(ui, mid=None):
                    _, hd, blk = units[ui]
                    par = ui % 2
                    p3 = ui % 3
                    gblk = half * NBH + blk
                    tpv = pb[3][0:64, :].cast(BF16).re("p (j d) -> p j d", j=8)
                    for j in range(8):
                        tr(tpv[:, j, :], vT[par][:, j * 64:(j + 1) * 64], identb)
                    cp("dve", vtok[par], tpv)
                    for j in range(8):
                        tr(tpv[:, j, :], kdec[par][:, j * 64:(j + 1) * 64], identb)
                    cp("dve", kdT, tpv)
                    psc = pb[4][0:64, :].re("p (j t) -> p j t", j=8)
                    for j in range(8):
                        mm(psc[:, j, :], ktil[par][:, j * 64:(j + 1) * 64], qtil[p3][:, j * 64:(j + 1) * 64])
                    tt("dve", scT[par], psc, tri64[0:64, :].re("p (o t) -> p o t", o=1).bc([64, 8, 64]), ALU.mult)
                    pkv = pb[5].re("p (j d) -> p j d", j=4)
                    last_unit_of_head = (blk == NBH - 1)
                    for hh in range(2):
                        for j4 in range(4):
                            j = hh * 4 + j4
                            mm(pkv[:, j4, :], kdT[:, j, :], vtok[par][:, j, :])
                        for j4 in range(4):
                            j = hh * 4 + j4
                            c = gblk * 8 + j
                            dcol = eq[par][:, j * 64 + 63:j * 64 + 64]
                            is_last = last_unit_of_head and j == 7
                            if is_last:
                                dst = Sfin[:, hd, :]
                            else:
                                dst = Sf[(c + 1) % 2]
                            if c == 0:
                                cp("dve", dst, pkv[:, j4, :])
                            else:
                                if j == 0 and blk == 0:
                                    prev = Sfin[:, hd, :]
                                else:
                                    prev = Sf[c % 2]
                                stt(dst, prev, dcol, pkv[:, j4, :], ALU.mult, ALU.add)
                            if not is_last:
                                if j < 7:
                                    cp("pool", SbA[ui % 3][:, j + 1, :], dst)
                                else:
                                    cp("pool", SbA[(ui + 1) % 3][:, 0, :], dst)
                        if hh == 0 and mid is not None:
                            mid()

                def pre_h(ui):
                    _, hd, blk = units[ui]
                    if half == 1 and blk == 0:
                        cp("pool", SbA[ui % 3][:, 0, :], Sfin[:, hd, :])

                def back_o(ui):
                    _, hd, blk = units[ui]
                    par = ui % 2
                    p3 = ui % 3
                    gblk = half * NBH + blk
                    po = pb[6]
                    for j in range(8):
                        c = gblk * 8 + j
                        first = (c == 0)
                        mm(po[:, j * 64:(j + 1) * 64], vtok[par][:, j, :], scT[par][:, j, :], start=True, stop=first)
                        if not first:
                            mm(po[:, j * 64:(j + 1) * 64], SbA[ui % 3][:, j, :], qtil[p3][:, j * 64:(j + 1) * 64],
                               start=False, stop=True)

                def back_n(ui):
                    _, hd, blk = units[ui]
                    par = ui % 2
                    p3 = ui % 3
                    po = pb[6]
                    act(osqb, po, AF.Square)
                    mm(pb[7], onesb, osqb)
                    act(sd, pb[7], AF.Ln, bias=EPS, scale=1.0 / 128)
                    act(sd, sd, AF.Exp, scale=-0.5)
                    stt(t1, po, cols[:, C_ON:C_ON + 1], sd, ALU.mult, ALU.mult)
                    tt("dve", V(outT_b[hd][blk], outT.ap[:, hd, blk * 512:(blk + 1) * 512]), t1, sg[p3], ALU.mult)

                def front_g(ui):
                    _, g, blk = units[ui]
                    if os.environ.get("KLIMIT"):
                        print("front_g start nrec", S.nrec)
                    load_slab("g", g)
                    if blk == 0 and not os.environ.get("NOPF"):
                        nk = next_slab_key(ui)
                        if nk is not None:
                            load_slab(*nk)
                    w = slab_of[("g", g)]
                    pvb = nextbank()
                    pv = pvb.re("p (j d) -> p j d", j=4)
                    for tl in range(4):
                        tile = hT_tiles[4 * blk + tl]
                        t0 = blk * 512 + tl * 128
                        for c in range(16):
                            mm(pv[:, tl, :], V(tile, hT.ap[:, c, t0:t0 + 128]), w[:, c, 128:256], start=(c == 0), stop=(c == 15))
                    s1, s2, mean, msq, rstd, nmr = st4
                    for tl in range(4):
                        act(vTf[:, tl * 128:(tl + 1) * 128], pv[:, tl, :], AF.Identity, accum=s1[:, tl:tl + 1], junk_out=True)
                        act(sqv[:, tl * 128:(tl + 1) * 128], pv[:, tl, :], AF.Square, accum=s2[:, tl:tl + 1], junk_out=True)
                    pu = nextbank()
                    inproj(pu, w, 0, blk)
                    ts("dve", mean, s1, 1.0 / 128, None, ALU.mult)
                    tt("dve", msq, mean, mean, ALU.mult)
                    stt(msq, s2, 1.0 / 128, msq, ALU.mult, ALU.subtract)
                    act(rstd, msq, AF.Sqrt, bias=EPS)
                    recip(rstd, rstd)
                    stt(nmr, mean, -1.0, rstd, ALU.mult, ALU.mult)
                    for tl in range(4):
                        act(vn[:, tl, :], pv[:, tl, :], AF.Identity, bias=nmr[:, tl:tl + 1], scale=rstd[:, tl:tl + 1])
                    pgt = nextbank()
                    inproj(pgt, w, 256, blk)
                    act(sgb, pgt, AF.Silu)
                    tt("dve", usg, pu, sgb, ALU.mult)
                    psv = pb[4]
                    for tl in range(4):
                        mm(psv[:, tl * 128:(tl + 1) * 128], vn[:, tl, :], wsT[:, g, :])
                    stt(svs.re("p (j t) -> p j t", j=4), psv.re("p (j t) -> p j t", j=4), cols[:, C_LNW + g:C_LNW + g + 1],
                        Cg[:, g, :].re("p (o t) -> p o t", o=1).bc([128, 4, 128]), ALU.mult, ALU.add)
                    tt("dve", V(outT_b[8 + g][blk], outT.ap[:, 8 + g, blk * 512:(blk + 1) * 512]), svs, usg, ALU.mult)

                n_u = len(units)
                if dbg == "B1":
                    units = units[:2] + units[16:18]
                    n_u = len(units)
                if dbg and dbg.startswith("B:"):
                    lo, hi = dbg[2:].split("-")
                    units = units[int(lo):int(hi)]
                    n_u = len(units)
                if dbg and dbg.startswith("BL:"):
                    units = [units[int(t)] for t in dbg[3:].split(",")]
                    n_u = len(units)
                for ui in range(n_u + 2):
                    if ui < n_u:
                        if units[ui][0] == "h":
                            front_a(ui)
                        else:
                            front_g(ui)
                    has_c = (2 <= ui and units[ui - 2][0] == "h")
                    if 1 <= ui <= n_u and units[ui - 1][0] == "h":
                        pre_h(ui - 1)
                        front_b(ui - 1, mid=((lambda u=ui - 2: back_o(u)) if has_c else None))
                        if has_c:
                            back_n(ui - 2)
                    elif has_c:
                        back_o(ui - 2)
                        back_n(ui - 2)

                if dbg and dbg[0] == "B":
                    dump(outT)
                phaseC_preload(ev_w_out)
                S.barrier()
                A.off = mark_xa
                XC = {"yacc": None,
                      "xt": xt4,
                      "ss8": A.alloc("ss8C", [TTH], F32),
                      "ssq": A.alloc("ssqC", [TTH, 4], F32),
                      "junk": A.alloc("junkC", [512], BF16)}
                yup = A.alloc("yaccU", [TTH // 2, D], F32)
                ytiles = []
                for i in range(TTH):
                    if i < TTH // 2:
                        ytiles.append(Buf(A.t[:, hT.off + i * D:hT.off + (i + 1) * D], "yaccL%d" % i))
                    else:
                        ytiles.append(Buf(yup.ap[:, i - TTH // 2, :], "yaccU%d" % i))
                XC["yacc"] = _Y(ytiles)
                S.mark("L0h%d:C" % half)
                if os.environ.get("KLIMIT"):
                    print("L0 C start nrec", S.nrec)
                phaseC(0, half, XC, ev_w_out, x1_d, is_final=(not do1),
                       after=(preload_next_A0 if half == 0 else None))
            A.off = mark_layer

        if do1 and not S.stopped:
            A.off = mark_layer
            kpeT = A.alloc("kpeT", [TFULL], BF16, parts=64)
            nc._kpeT_off = kpeT.off
            invrow = A.alloc("invrow", [64], F32, parts=1)
            invcol = A.alloc("invcol", [1], F32, parts=64)
            lat_off = A.off
            cqsT = A.alloc("cqsT", [4, TH], BF16)
            ckvsT = A.alloc("ckvsT", [4, TH], BF16)
            rstdq = A.alloc("rstdq", [TH], F32)
            rstdkv = A.alloc("rstdkv", [TH], F32)
            Ctab = A.alloc("Ctab", [TH], F32, parts=64)
            Stab = A.alloc("Stab", [TH], F32, parts=64)
            assert A.off - lat_off == 8192
            mark_x1 = A.off

            for p in range(64):
                val = float(np.float32(10000.0 ** (-(p % 32) / 32.0))) / (2.0 * math.pi)
                memset("pool", invrow[0:1, p:p + 1], val)
            inv_d = nc.dram_tensor("invs", [64], F32).ap()
            inv_b = Buf(None, "invs")
            dma_out("sp", inv_d.rearrange("(o n) -> o n", o=1), invrow, writes=[inv_b])
            dma_in("sp", invcol, inv_d.rearrange("(p x) -> p x", x=1), reads=[inv_b])

            w1v = od_w_in.rearrange("(c p) n -> p c n", p=128)
            wqv = od_wqb.rearrange("(c p) n -> p c n", p=128)
            wkvv = od_wkvb.rearrange("(c p) n -> p c n", p=128)
            SCALE = 192.0 ** -0.5

            A.off = mark_x1
            xt4 = [A.alloc("xt4b_%d" % i, [D], F32) for i in range(6)]
            mark_xa1 = A.off

            def preload_next_A1():
                for j in range(4):
                    dma_in("sp", xt4[2 + j], x1_d[(TTH + j) * 128:(TTH + j + 1) * 128, :], reads=[x1_tiles[TTH + j]])

            for half in range(2):
                dma_in("pool", wslab[0], w1v[:, :, 0:512])
                dma_in("pool", wslab[1], w1v[:, :, 512:1024])
                S.barrier()
                A.off = mark_xa1
                XA = {"xt": (xt4[0:4] if half == 0 else xt4[2:6]),
                      "hb": [A.alloc("hbA%d" % i, [D], BF16) for i in range(2)],
                      "junk": A.alloc("junkA", [D], BF16),
                      "ss8": A.alloc("ss8A", [TTH], F32)}
                S.mark("L1h%d:A" % half)
                phaseA(1, half, XA, preloaded=(4 if half == 1 else 0))
                S.mark("L1h%d:B0" % half)
                if dbg == "L1:%d:A" % half:
                    S.stopped = True
                if os.environ.get("KLIMIT"):
                    print("L1 B0 start nrec", S.nrec)
                S.barrier()
                A.off = mark_x1
                f2 = lambda n: A.alloc(n, [512], F32)
                rkvcol = A.alloc("rkvcol", [TTH], F32)
                CRq = A.alloc("CRq", [TH], F32, parts=64)
                SRq = A.alloc("SRq", [TH], F32, parts=64)
                tA = A.alloc("tA", [512], F32, parts=64)
                tB = A.alloc("tB", [512], F32, parts=64)
                wq = [A.alloc("wq%d" % i, [4, 256], BF16) for i in range(2)]
                wkv = [A.alloc("wkv%d" % i, [4, 256], BF16) for i in range(2)]
                mark_b1 = A.off

                def load_w(h):
                    if h >= 16:
                        return
                    if h % 4 == 0:
                        hg = h // 4
                        dma_in("pool", wslab[hg % 2], w1v[:, :, 1088 + hg * 512:1088 + (hg + 1) * 512])
                    w_ = wq[h % 2]
                    dma_in("pool", w_[:, :, 0:192], wqv[:, :, h * 192:(h + 1) * 192])
                    dma_in("pool", w_[:, :, 192:224], wqv[:, :, h * 192 + 160:h * 192 + 192], nowaw=True)
                    dma_in("pool", w_[:, :, 224:256], wqv[:, :, h * 192 + 128:h * 192 + 160], nowaw=True)
                    dma_in("pool", wkv[h % 2], wkvv[:, :, h * 256:(h + 1) * 256])

                wkpe = A.alloc("wkpe", [16, 128], BF16)
                sqA = [A.alloc("sqA%d" % i, [512], BF16) for i in range(2)]
                sdl = f2("sdl")
                posi = A.alloc("posi", [TH], I32, parts=64)
                uu = A.alloc("uu", [TH], F32, parts=64)
                ki = A.alloc("ki", [TH], I32, parts=64)
                kf = A.alloc("kf", [TH], F32, parts=64)

                dma_in("sp", posi, pos_d[half * TH:(half + 1) * TH].partition_broadcast(64))
                cp("dve", uu, posi)
                ts("dve", uu, uu, invcol[:, 0:1], None, ALU.mult)
                for tab, shift in ((Stab, 0.0), (Ctab, 0.25)):
                    if shift:
                        ts("dve", kf, uu, shift, None, ALU.add)
                        src_u = kf
                    else:
                        src_u = uu
                    cp("dve", ki, src_u)
                    cp("dve", tab, ki)
                    tt("dve", tab, src_u, tab, ALU.subtract)
                    ts("dve", kf, tab, 0.5, None, ALU.is_gt)
                    tt("dve", tab, tab, kf, ALU.subtract)
                    act(tab, tab, AF.Sin, scale=6.283185)
                ts("dve", Stab[0:32, :], Stab[0:32, :], -1.0, None, ALU.mult)

                if os.environ.get("KLIMIT"):
                    print("L1 B0 tables done nrec", S.nrec)
                dma_in("pool", wkpe[:, :, 0:64], w1v[:, :, 1024:1088])
                dma_in("pool", wkpe[:, :, 64:96], w1v[:, :, 1056:1088], nowaw=True)
                dma_in("pool", wkpe[:, :, 96:128], w1v[:, :, 1024:1056], nowaw=True)
                for blk in range(NBH):
                    tok = slice(blk * 512, (blk + 1) * 512)
                    gtok = slice(half * TH + blk * 512, half * TH + (blk + 1) * 512)
                    for (wsl, dstT, rst, ncol0) in ((wslab[0], cqsT, rstdq, C_QN), (wslab[1], ckvsT, rstdkv, C_KVN)):
                        for g in range(4):
                            bank = nextbank()
                            inproj(bank, wsl, g * 128, blk)
                            act(sqA[g % 2], bank, AF.Square)
                            act(dstT[:, g, tok], bank, AF.Identity, scale=cols[:, ncol0 + g:ncol0 + g + 1])
                            mm(pb[7], onesb, sqA[g % 2], start=(g == 0), stop=(g == 3))
                        act(sdl, pb[7], AF.Ln, bias=EPS, scale=1.0 / 512)
                        act(rst[:, tok], sdl, AF.Exp, scale=-0.5)
                    for tl in range(4):
                        tr(pb[6][:, tl * 128:(tl + 1) * 128], rstdkv[:, blk * 512 + tl * 128:blk * 512 + (tl + 1) * 128], identf)
                    cp("dve", rkvcol[:, blk * 4:(blk + 1) * 4], pb[6].re("p (j d) -> p j d", j=4)[:, :, 0])
                    bka = nextbank()
                    inproj(bka[0:64, :], wkpe, 0, blk, m=64)
                    bkb = nextbank()
                    inproj(bkb[0:64, :], wkpe, 64, blk, m=64)
                    tt("dve", tA, bka[0:64, :], Ctab[:, tok], ALU.mult)
                    tt("dve", tB, bkb[0:64, :], Stab[:, tok], ALU.mult)
                    tt("dve", kpeT[:, gtok], tA, tB, ALU.add)
                    tt("dve", CRq[:, tok], Ctab[:, tok], rstdq[0:64, tok], ALU.mult)
                    tt("dve", SRq[:, tok], Stab[:, tok], rstdq[0:64, tok], ALU.mult)

                load_w(0)
                if dbg == "L1:%d:B0" % half:
                    S.stopped = True
                S.barrier()
                A.off = mark_b1
                qnT = [A.alloc("qnT%d" % i, [TH], BF16) for i in range(2)]
                qrT = [A.alloc("qrT%d" % i, [TH], BF16, parts=64) for i in range(2)]
                knT = [A.alloc("knT%d" % i, [TH], BF16) for i in range(2)]
                Vtok = [A.alloc("Vtok%d" % i, [TTH, 128], BF16) for i in range(2)]
                knP = [A.alloc("knP%d" % i, [TH], BF16) for i in range(2)]
                VP = [A.alloc("VP%d" % i, [TTH, 128], BF16) for i in range(2)]
                sgate = [A.alloc("sgate%d" % i, [NBH, 512], F32) for i in range(2)]
                Eb = [A.alloc("Eb%d" % i, [512], BF16) for i in range(4)]
                rden = f2("rden")
                pbanks = [pb[0], pb[1], pb[2]]
                sbanks = [pb[3], pb[4], pb[7]]
                prot = [0]

                def pbank():
                    b = pbanks[prot[0] % 3]
                    prot[0] += 1
                    return b

                def proj(h):
                    p = h % 2
                    hg, hh = h // 4, h % 4
                    gslab = wslab[hg % 2]
                    load_w(h + 1)
                    if half == 1:
                        dma_in("sp", knP[p], kvs_d[h, 0], reads=[kvs_bufs[h]])
                        dma_in("sp", VP[p].re("p a b -> p (a b)"), kvs_d[h, 1], reads=[kvs_bufs[h]])
                    for blk in range(NBH):
                        tok = slice(blk * 512, (blk + 1) * 512)
                        bank = pbank()
                        for rc in range(4):
                            mm(bank, wq[p][:, rc, 0:128], cqsT[:, rc, tok], start=(rc == 0), stop=(rc == 3))
                        tt("dve", qnT[p][:, tok], bank, rstdq[:, tok], ALU.mult)
                        bka = pbank()
                        for rc in range(4):
                            mm(bka[0:64, :], wq[p][:, rc, 128:192], cqsT[:, rc, tok], start=(rc == 0), stop=(rc == 3))
                        bkb = pbank()
                        for rc in range(4):
                            mm(bkb[0:64, :], wq[p][:, rc, 192:256], cqsT[:, rc, tok], start=(rc == 0), stop=(rc == 3))
                        tt("dve", tA, bka[0:64, :], CRq[:, tok], ALU.mult)
                        tt("dve", tB, bkb[0:64, :], SRq[:, tok], ALU.mult)
                        tt("pool", qrT[p][:, tok], tA, tB, ALU.add)
                        bank = pbank()
                        for rc in range(4):
                            mm(bank, wkv[p][:, rc, 0:128], ckvsT[:, rc, tok], start=(rc == 0), stop=(rc == 3))
                        tt("dve", knT[p][:, tok], bank, rstdkv[:, tok], ALU.mult)
                        bank = pbank()
                        bv = bank.re("p (j d) -> p j d", j=4)
                        for tl in range(4):
                            t0 = blk * 512 + tl * 128
                            for rc in range(4):
                                mm(bv[:, tl, :], ckvsT[:, rc, t0:t0 + 128], wkv[p][:, rc, 128:256],
                                   start=(rc == 0), stop=(rc == 3))
                        for tl in range(4):
                            ti = blk * 4 + tl
                            act(Vtok[p][:, ti, :], bv[:, tl, :], AF.Copy, scale=rkvcol[:, ti:ti + 1])
                        bank = pbank()
                        inproj(bank, gslab, hh * 128, blk)
                        act(sgate[p][:, blk, :], bank, AF.Silu)
                    if half == 0:
                        dma_out("sp", kvs_d[h, 0], knT[p], writes=[kvs_bufs[h]])
                        dma_out("sp", kvs_d[h, 1], Vtok[p].re("p a b -> p (a b)"), writes=[kvs_bufs[h]])

                def attn(h):
                    p = h % 2
                    for qb in range(NBH):
                        qtok0 = qb * 512
                        tiles = []
                        if half == 1:
                            for kt in range(TTH):
                                tiles.append(("p", kt, 0))
                        for kt in range(4 * qb + 4):
                            r = kt - 4 * qb
                            tiles.append(("l", kt, max(r, -1)))
                        po = pb[5]
                        pd = pb[6]
                        nt = len(tiles)
                        Es = [None] * nt

                        def emit_s(i):
                            kind, kt, r = tiles[i]
                            c0 = 128 * r if r > 0 else 0
                            ps = sbanks[i % 3]
                            if kind == "p":
                                kn_src, kg = knP[p], kt
                            else:
                                kn_src, kg = knT[p], half * TTH + kt
                            mm(ps[:, c0:512], kn_src[:, kt * 128:(kt + 1) * 128], qnT[p][:, qtok0 + c0:qtok0 + 512], start=True, stop=False)
                            mm(ps[:, c0:512], kpeT[:, kg * 128:(kg + 1) * 128], qrT[p][:, qtok0 + c0:qtok0 + 512], start=False, stop=True)
                            E = Eb[i % 4]
                            act(E[:, c0:512], ps[:, c0:512], AF.Exp, scale=SCALE)
                            if kind == "l" and r >= 0:
                                tt("pool", E[:, c0:c0 + 128], E[:, c0:c0 + 128], tri128b, ALU.mult)
                            Es[i] = (E, c0)

                        def emit_pv(i):
                            kind, kt, r = tiles[i]
                            E, c0 = Es[i]
                            v_src = VP[p] if kind == "p" else Vtok[p]
                            mm(po[:, c0:512], v_src[:, kt, :], E[:, c0:512], start=(i == 0), stop=(i == nt - 1))
                            mm(pd[:, c0:512], onesb, E[:, c0:512], start=(i == 0), stop=(i == nt - 1))

                        emit_s(0)
                        if nt > 1:
                            emit_s(1)
                        for i in range(nt):
                            if i + 2 < nt:
                                emit_s(i + 2)
                            emit_pv(i)
                        act(rden, pd, AF.Ln)
                        act(rden, rden, AF.Exp, scale=-1.0)
                        tt("dve", rden, po, rden, ALU.mult)
                        tt("dve", V(outT_b[h][qb], outT.ap[:, h, qtok0:qtok0 + 512]), rden, sgate[p][:, qb, :], ALU.mult)

                S.mark("L1h%d:B1" % half)
                proj(0)
                for h in range(16):
                    if h + 1 < 16:
                        proj(h + 1)
                    attn(h)

                if dbg == "L1B" or dbg == "L1:%d:B1" % half:
                    S.stopped = True
                phaseC_preload(od_w_out)
                S.barrier()
                A.off = mark_xa1
                XC = {"yacc": None,
                      "xt": xt4,
                      "ss8": A.alloc("ss8C", [TTH], F32),
                      "ssq": A.alloc("ssqC", [TTH, 4], F32),
                      "junk": A.alloc("junkC", [512], BF16)}
                ytiles = []
                for i in range(TTH):
                    if i < TTH // 2:
                        ytiles.append(Buf(A.t[:, hT.off + i * D:hT.off + (i + 1) * D], "yaccL%d" % i))
                    else:
                        j = i - TTH // 2
                        ytiles.append(Buf(A.t[:, lat_off + j * D:lat_off + (j + 1) * D], "yaccU%d" % i))
                XC["yacc"] = _Y(ytiles)
                S.mark("L1h%d:C" % half)
                phaseC(1, half, XC, od_w_out, out_d, is_final=True,
                       after=(preload_next_A1 if half == 0 else None))
                if dbg == "L1:%d:C" % half:
                    S.stopped = True

        S.mark("END")
        S.emit(nc, st)
    nc._marks = S.marks
    return nc


_PROG = {}


def _get_prog(layers):
    if layers not in _PROG:
        _PROG[layers] = build_program(layers)
    return _PROG[layers]


def _in_maps_l0(inputs, xs):
    maps = []
    for c in range(8):
        b = c % 4
        maps.append({
            "x": np.ascontiguousarray(xs[b]),
            "norm_pre": np.ascontiguousarray(inputs["norm_pre"]),
            "norm_post": np.ascontiguousarray(inputs["norm_post"]),
            "ev_w_in": np.ascontiguousarray(inputs["ev_w_in"][0]),
            "ev_lb_logits": np.ascontiguousarray(inputs["ev_lb_logits"].reshape(16, 128)),
            "ev_a_onorm": np.ascontiguousarray(inputs["ev_a_onorm"].reshape(1, 128)),
            "ev_b_ln_w": np.ascontiguousarray(inputs["ev_b_ln_w"].reshape(8, 128)),
            "ev_b_ln_b": np.ascontiguousarray(inputs["ev_b_ln_b"].reshape(8, 128)),
            "ev_b_ws": np.ascontiguousarray(inputs["ev_b_ws"][0]),
            "ev_b_bias": np.ascontiguousarray(inputs["ev_b_bias"].reshape(1024)),
            "ev_w_out": np.ascontiguousarray(inputs["ev_w_out"][0]),
        })
    return maps


def _in_maps_l1(inputs, xs):
    maps = []
    for c in range(8):
        b = c % 4
        maps.append({
            "x": np.ascontiguousarray(xs[b]),
            "positions": np.ascontiguousarray(inputs["positions"][b]).astype(np.int32),
            "norm_pre": np.ascontiguousarray(inputs["norm_pre"]),
            "norm_post": np.ascontiguousarray(inputs["norm_post"]),
            "od_w_in": np.ascontiguousarray(inputs["od_w_in"][0]),
            "od_q_norm": np.ascontiguousarray(inputs["od_q_norm"].reshape(4, 128)),
            "od_w_qb": np.ascontiguousarray(inputs["od_w_qb"][0]),
            "od_kv_norm": np.ascontiguousarray(inputs["od_kv_norm"].reshape(4, 128)),
            "od_w_kvb": np.ascontiguousarray(inputs["od_w_kvb"][0]),
            "od_w_out": np.ascontiguousarray(inputs["od_w_out"][0]),
        })
    return maps


FUSED = True


def kernel(**inputs):
    inputs = {k: np.asarray(v) for k, v in inputs.items()}
    if FUSED:
        nc = _get_prog((0, 1))
        m0 = _in_maps_l0(inputs, inputs["x"])
        m1 = _in_maps_l1(inputs, inputs["x"])
        maps = [dict(a, **b) for a, b in zip(m1, m0)]
        res = run_bass_kernel_spmd(nc, maps, core_ids=list(range(8)))
        return np.stack([res.results[b]["out"] for b in range(4)]).astype(np.float32)
    nc0 = _get_prog((0,))
    res0 = run_bass_kernel_spmd(nc0, _in_maps_l0(inputs, inputs["x"]), core_ids=list(range(8)))
    x1 = [res0.results[b]["out"] for b in range(4)]
    nc1 = _get_prog((1,))
    res1 = run_bass_kernel_spmd(nc1, _in_maps_l1(inputs, x1), core_ids=list(range(8)))
    return np.stack([res1.results[b]["out"] for b in range(4)]).astype(np.float32)


if __name__ == "__main__":
    import time
    t0 = time.time()
    nc = build_program((0,))
    print("built", time.time() - t0)
```
